# Optimizing a Trainium2 kernel written in Bass

```python
import jax, jax.numpy as jnp
from jax import lax
import numpy as np

D_MODEL = 2048
BATCH = 2
SEQ = 8192
DEPTH = 2

D_MIX = D_MODEL
D_ATTN = D_MIX // 2
D_CONV = D_MIX // 4
D_POOL = D_MIX // 4
HEAD_DIM = 64
ATTN_HEADS = D_ATTN // HEAD_DIM
CONV_GROUPS = 8
CONV_WIDTH = 3
POOL_WINDOWS = (2, 4, 8, 16)
POOL_GROUP = D_POOL // len(POOL_WINDOWS)
DILATED_PATTERNS = ((128, 1), (512, 4), (2048, 16))
BLK = 128
D_IN = 3 * D_ATTN + 3 * D_CONV + D_POOL
D_FF = 5632
FFN_RESIDUAL = 0.5
RMS_EPS = 1e-6
NEG_INF = -1e30

kernel_name = "hybrid_dilated_conv_pool_macaron"


def rmsnorm(x, g):
    xf = x.astype(jnp.float32)
    y = xf * lax.rsqrt(jnp.mean(xf * xf, axis=-1, keepdims=True) + RMS_EPS)
    return (y * g.astype(jnp.float32)).astype(x.dtype)


def swiglu(h, w_gate, w_up, w_down):
    return (jax.nn.silu(h @ w_gate) * (h @ w_up)) @ w_down


def dilated_band_attention(q, k, v, window, dilation):
    B, S, H, Dh = q.shape
    d = dilation
    L = S // d
    span = window // d
    assert span <= BLK
    nb = -(-L // BLK)
    Lp = nb * BLK

    def to_sub(t):
        t = t.reshape(B, L, d, H, Dh).transpose(0, 2, 1, 3, 4).reshape(B * d, L, H, Dh)
        t = jnp.pad(t, ((0, 0), (0, Lp - L), (0, 0), (0, 0)))
        return t.reshape(B * d, nb, BLK, H, Dh)

    def with_prev(t):
        prev = jnp.pad(t, ((0, 0), (1, 0), (0, 0), (0, 0), (0, 0)))[:, :-1]
        return jnp.concatenate([prev, t], axis=2)

    qb = to_sub(q)
    kc = with_prev(to_sub(k))
    vc = with_prev(to_sub(v))
    s = jnp.einsum('znqhd,znkhd->znhqk', qb, kc).astype(jnp.float32) * (Dh ** -0.5)
    qi = jnp.arange(BLK)[:, None]
    kj = jnp.arange(2 * BLK)[None, :]
    dist = qi + BLK - kj
    blk = jnp.arange(nb)[:, None, None]
    valid = (dist >= 0) & (dist <= span) & (blk * BLK + kj - BLK >= 0)
    s = jnp.where(valid[None, :, None], s, NEG_INF)
    m = jnp.max(s, axis=-1, keepdims=True)
    p = jnp.exp(s - m)
    l = jnp.sum(p, axis=-1)
    num = jnp.einsum('znhqk,znkhd->znqhd', p.astype(v.dtype), vc).astype(jnp.float32)
    l_q = l.transpose(0, 1, 3, 2)
    o = num / l_q[..., None]
    lse = m[..., 0].transpose(0, 1, 3, 2) + jnp.log(l_q)

    def from_sub(t):
        rest = t.shape[3:]
        t = t.reshape((B * d, Lp) + rest)[:, :L]
        t = t.reshape((B, d, L) + rest)
        t = jnp.moveaxis(t, 1, 2)
        return t.reshape((B, S) + rest)

    return from_sub(o), from_sub(lse)


def dilated_attention(q, k, v):
    outs, lses = [], []
    for window, dilation in DILATED_PATTERNS:
        o, lse = dilated_band_attention(q, k, v, window, dilation)
        outs.append(o)
        lses.append(lse)
    w = jax.nn.softmax(jnp.stack(lses, axis=0), axis=0)
    o = jnp.sum(w[..., None] * jnp.stack(outs, axis=0), axis=0)
    return o.astype(q.dtype)


def causal_short_conv(u, conv_w):
    S = u.shape[1]
    up = jnp.pad(u, ((0, 0), (CONV_WIDTH - 1, 0), (0, 0)))
    return conv_w[0] * up[:, 0:S] + conv_w[1] * up[:, 1:S + 1] + conv_w[2] * up[:, 2:S + 2]


def multiscale_pool(u, pool_w, pool_scale):
    B, S, C = u.shape
    uf = u.astype(jnp.float32)
    csz = jnp.pad(jnp.cumsum(uf, axis=1), ((0, 0), (1, 0), (0, 0)))
    pos = jnp.arange(S)
    outs = []
    for g, w in enumerate(POOL_WINDOWS):
        sl = slice(g * POOL_GROUP, (g + 1) * POOL_GROUP)
        P = jnp.pad(csz[..., sl], ((0, 0), (w - 1, 0), (0, 0)))
        win_sum = P[:, w:w + S] - P[:, 0:S]
        count = jnp.minimum(pos + 1, w).astype(jnp.float32)[None, :, None]
        outs.append(win_sum / count - uf[..., sl])
    pooled = jnp.stack(outs, axis=2).astype(u.dtype)
    y = jnp.einsum('bsgc,gcd->bsgd', pooled, pool_w).reshape(B, S, C)
    return y * pool_scale


def hybrid_mixer(h, w_in, conv_w, pool_w, pool_scale, w_out):
    B, S, _ = h.shape
    z = h @ w_in
    cuts = np.cumsum([D_ATTN, D_ATTN, D_ATTN, D_CONV, D_CONV, D_CONV])
    q, k, v, gate_b, gate_c, conv_in, pool_in = jnp.split(z, cuts, axis=-1)
    heads = lambda t: t.reshape(B, S, ATTN_HEADS, HEAD_DIM)
    y_attn = dilated_attention(heads(q), heads(k), heads(v)).reshape(B, S, D_ATTN)
    y_conv = gate_b * causal_short_conv(gate_c * conv_in, conv_w)
    y_pool = multiscale_pool(pool_in, pool_w, pool_scale)
    return jnp.concatenate([y_attn, y_conv, y_pool], axis=-1) @ w_out


def setup_inputs(seed: int = 0) -> dict:
    key = jax.random.key(seed)
    ks = jax.random.split(key, 16)
    f32 = jnp.float32

    def lin(k, shape, fan_in):
        return jax.random.normal(k, shape, f32) * (fan_in ** -0.5)

    def gain(k, shape, noise=0.02):
        return 1.0 + noise * jax.random.normal(k, shape, f32)

    return {
        "x": jax.random.normal(ks[0], (BATCH, SEQ, D_MODEL), f32),
        "ffn1_norm": gain(ks[1], (DEPTH, D_MODEL)),
        "ffn1_w_gate": lin(ks[2], (DEPTH, D_MODEL, D_FF), D_MODEL),
        "ffn1_w_up": lin(ks[3], (DEPTH, D_MODEL, D_FF), D_MODEL),
        "ffn1_w_down": lin(ks[4], (DEPTH, D_FF, D_MODEL), D_FF),
        "mix_norm": gain(ks[5], (DEPTH, D_MODEL)),
        "w_in": lin(ks[6], (DEPTH, D_MODEL, D_IN), D_MODEL),
        "conv_w": lin(ks[7], (DEPTH, CONV_WIDTH, D_CONV), CONV_WIDTH),
        "pool_w": lin(ks[8], (DEPTH, len(POOL_WINDOWS), POOL_GROUP, POOL_GROUP), POOL_GROUP),
        "pool_scale": gain(ks[9], (DEPTH, D_POOL), 0.1),
        "w_out": lin(ks[10], (DEPTH, D_MIX, D_MODEL), D_MIX),
        "ffn2_norm": gain(ks[11], (DEPTH, D_MODEL)),
        "ffn2_w_gate": lin(ks[12], (DEPTH, D_MODEL, D_FF), D_MODEL),
        "ffn2_w_up": lin(ks[13], (DEPTH, D_MODEL, D_FF), D_MODEL),
        "ffn2_w_down": lin(ks[14], (DEPTH, D_FF, D_MODEL), D_FF),
        "final_norm": gain(ks[15], (D_MODEL,)),
    }


def reference(x, ffn1_norm, ffn1_w_gate, ffn1_w_up, ffn1_w_down, mix_norm, w_in, conv_w,
              pool_w, pool_scale, w_out, ffn2_norm, ffn2_w_gate, ffn2_w_up, ffn2_w_down,
              final_norm):
    for l in range(DEPTH):
        x = x + FFN_RESIDUAL * swiglu(rmsnorm(x, ffn1_norm[l]), ffn1_w_gate[l], ffn1_w_up[l], ffn1_w_down[l])
        x = x + hybrid_mixer(rmsnorm(x, mix_norm[l]), w_in[l], conv_w[l], pool_w[l], pool_scale[l], w_out[l])
        x = x + FFN_RESIDUAL * swiglu(rmsnorm(x, ffn2_norm[l]), ffn2_w_gate[l], ffn2_w_up[l], ffn2_w_down[l])
    return rmsnorm(x, final_norm)
```

```python
import numpy as np
import ml_dtypes
from contextlib import ExitStack
import concourse.bass as bass
import concourse.mybir as mybir
from concourse.bass_utils import run_bass_kernel_spmd

F32 = mybir.dt.float32
BF16 = mybir.dt.bfloat16
AF = mybir.ActivationFunctionType
ALU = mybir.AluOpType

D = 2048
NT = 2048
TT = 1024
DFF = 5632
DIN = 5120
NCORES = 8
EPS = 1e-6
ENGS = ["pe", "act", "dve", "pool", "sp"]
FGROUPS = [(0, 12), (12, 12), (24, 12), (36, 8)]


class Sem:
    def __init__(self, h, name):
        self.h, self.n, self.name = h, 0, name


class Builder:
    def __init__(self, nc, stack):
        self.nc, self.stack = nc, stack
        self.streams = {e: [] for e in ENGS}
        self.cnt = {e: self.sem("c_" + e) for e in ["pe", "act", "dve", "pool"]}
        self.arena_words = 51 * 1024 + 512
        self.arena = stack.enter_context(nc.sbuf_tensor("arena", [128, self.arena_words], F32))
        self.aoff = 0
        self.ps = stack.enter_context(nc.psum_tensor("ps", [128, 8, 512], F32))
        self.pair_free = [None] * 4
        self.pair_i = 0
        self.nsem = 0

    def sem(self, name):
        if not hasattr(self, "_sems"):
            self._sems = {}
        if name not in self._sems:
            self._sems[name] = Sem(self.stack.enter_context(self.nc.semaphore(name)), name)
        return self._sems[name]

    def full_barrier(self):
        toks = [(s_, s_.n) for s_ in self._sems.values() if s_.n > 0]
        for e in ENGS:
            self.wait_only(e, toks)
        self.pair_free = [None] * 4
        for nm in ("wgu_free", "wd_free", "sg_free", "stg_free", "vstg_free"):
            if hasattr(self, nm):
                setattr(self, nm, [None, None])
        if hasattr(self, "sg_free"):
            self.sq_free = self.sg_free
        self.x_free = []
        self.acc_free = None
        self.rstd_free = None
        self.h_free = None

    def alloc(self, words):
        off = self.aoff
        self.aoff += words
        assert self.aoff <= self.arena_words, (self.aoff, self.arena_words)
        return self.arena[:, off:off + words]

    def alloc_f32(self, *shape):
        n = int(np.prod(shape))
        ap = self.alloc(n)
        if len(shape) == 2:
            return ap.rearrange("p (a b) -> p a b", a=shape[0])
        if len(shape) == 3:
            return ap.rearrange("p (a b c) -> p a b c", a=shape[0], b=shape[1])
        return ap

    def alloc_bf16(self, *shape):
        n = int(np.prod(shape))
        assert n % 2 == 0
        ap = self.alloc(n // 2).bitcast(BF16)
        if len(shape) == 2:
            return ap.rearrange("p (a b) -> p a b", a=shape[0])
        if len(shape) == 3:
            return ap.rearrange("p (a b c) -> p a b c", a=shape[0], b=shape[1])
        return ap

    def op(self, eng, fn, waits=(), post=True):
        sem = None
        tok = None
        if post:
            sem = self.cnt[eng]
            sem.n += 1
            tok = (sem, sem.n)
        self.streams[eng].append((fn, [w for w in waits if w is not None], sem, 1))
        return tok

    def dma(self, eng, out, in_, sem, waits=()):
        sem.n += 16
        self.streams[eng].append(
            (lambda e, out=out, in_=in_: e.dma_start(out=out, in_=in_),
             [w for w in waits if w is not None], sem, 16))
        return (sem, sem.n)

    def wait_only(self, eng, waits):
        self.streams[eng].append((None, [w for w in waits if w is not None], None, 0))

    def get_pair(self):
        k = self.pair_i % 4
        self.pair_i += 1
        return k, self.ps[:, 2 * k:2 * k + 2, :], self.pair_free[k]

    def barrier(self, extra=()):
        toks = [(s, s.n) for s in self.cnt.values() if s.n > 0] + list(extra)
        for e in ENGS:
            self.wait_only(e, toks)

    def emit(self):
        nc = self.nc
        with nc.Block() as block:
            def mk(name):
                items = self.streams[name]

                def run(e):
                    seen = {}
                    for fn, waits, sem, inc in items:
                        for (s, v) in waits:
                            if seen.get(s.name, 0) < v:
                                e.wait_ge(s.h, v)
                                seen[s.name] = v
                        if fn is not None:
                            ins = fn(e)
                            if sem is not None:
                                ins.then_inc(sem.h, inc)
                return run
            block.tensor(mk("pe"))
            block.scalar(mk("act"))
            block.vector(mk("dve"))
            block.gpsimd(mk("pool"))
            block.sync(mk("sp"))

    def setup_ffn_bufs(self):
        self.X = self.alloc_f32(16, TT)
        self.H = self.alloc_bf16(16, TT)
        self.AT = self.alloc_bf16(12, TT)
        self.WG = [self.alloc_bf16(16, 256) for _ in range(2)]
        self.WU = [self.alloc_bf16(16, 256) for _ in range(2)]
        self.WD = [self.alloc_bf16(12, 512) for _ in range(2)]
        self.ACC = self.alloc_f32(2, 512)
        self.RSTD = self.alloc_f32(2, 512)
        self.SG = [self.alloc(512) for _ in range(2)]
        self.SQ = self.SG
        self.VSTG = [self.alloc(260).bitcast(BF16).rearrange("p (h c) -> p h c", c=65) for _ in range(2)]
        self.tok_vstg = None
        for i_ in range(2):
            self.tok_vstg = self.op("dve", lambda e, i_=i_: e.memset(self.VSTG[i_], 1.0))
        self.ones = self.alloc(128)
        self.s_wgu = [self.sem("wgu%d" % i) for i in range(2)]
        self.s_wd = [self.sem("wd%d" % i) for i in range(2)]
        self.s_xld = self.sem("xld")
        self.s_xst = self.sem("xst")
        self.wgu_free = [None, None]
        self.wd_free = [None, None]
        self.wgu_i = getattr(self, "wgu_i", 0)
        self.wd_i = getattr(self, "wd_i", 0)
        self.sg_free = [None, None]
        self.sq_free = self.sg_free
        self.x_free = []
        self.tok_ones = self.op("dve", lambda e: e.memset(self.ones, 1.0))

    def load_x(self, src, tok0):
        toks = []
        for i in range(4):
            t = self.dma("sp", self.X[:, 4 * i:4 * i + 4, :],
                         src[i * 512:(i + 1) * 512, tok0:tok0 + TT].rearrange("(c p) t -> p c t", p=128),
                         self.s_xld, waits=list(self.x_free) if i == 0 else ())
            toks.append(t)
        return toks[-1]

    def store_x(self, dst, tok0, waits):
        t = None
        for i in range(4):
            t = self.dma("sp", dst[i * 512:(i + 1) * 512, tok0:tok0 + TT].rearrange("(c p) t -> p c t", p=128),
                         self.X[:, 4 * i:4 * i + 4, :], self.s_xst, waits=waits if i == 0 else ())
        return t

    def norm_tile_old(self, gain, x_ready):
        X, H, ACC, RSTD = self.X, self.H, self.ACC, self.RSTD
        last_acc = None
        for tb in range(2):
            sl = slice(tb * 512, (tb + 1) * 512)
            for c in range(16):
                if c == 0:
                    t_sq = self.op("act", lambda e, c=c, sl=sl, tb=tb: e.activation(
                        out=ACC[:, tb, :], in_=X[:, c, sl], func=AF.Square),
                        waits=[x_ready, last_acc if tb == 0 else None])
                    last = t_sq
                else:
                    s = c % 2
                    t_sq = self.op("act", lambda e, c=c, sl=sl, s=s: e.activation(
                        out=self.SQ[s], in_=X[:, c, sl], func=AF.Square),
                        waits=[x_ready, self.sq_free[s]])
                    last = self.op("dve", lambda e, tb=tb, s=s: e.tensor_tensor(
                        out=ACC[:, tb, :], in0=self.SQ[s], in1=ACC[:, tb, :], op=ALU.add),
                        waits=[t_sq, last])
                    self.sq_free[s] = last
            last_acc = last
        k, pp, pfree = self.get_pair()
        t_pe = None
        for tb in range(2):
            t_pe = self.op("pe", lambda e, tb=tb, pp=pp: e.matmul(
                pp[:, tb, :], self.ones.rearrange("p (a b) -> p a b", a=1)[:, 0, :], ACC[:, tb, :],
                start=True, stop=True), waits=[last_acc, pfree, self.tok_ones])
        t_act = self.op("act", lambda e, pp=pp: e.activation(
            out=RSTD, in_=pp, func=AF.Sqrt, bias=self.eps_ap, scale=1.0 / D), waits=[t_pe])
        self.pair_free[k] = t_act
        t_r = self.op("dve", lambda e: e.reciprocal(out=RSTD, in_=RSTD), waits=[t_act])
        t_h = None
        for c in range(16):
            t_h = self.op("dve", lambda e, c=c: e.scalar_tensor_tensor(
                out=H[:, c, :], in0=X[:, c, :], scalar=gain[:, c:c + 1],
                in1=RSTD.rearrange("p a b -> p (a b)"), op0=ALU.mult, op1=ALU.mult),
                waits=[t_r, x_ready])
        return t_h

    def load_wgu(self, wg, wu, f0):
        return self.load_pair(wg[:, f0:f0 + 256], wu[:, f0:f0 + 256])

    def load_wgu_old(self, wg, wu, f0):
        s = self.wgu_i % 2
        self.wgu_i += 1
        w = [self.wgu_free[s]]
        self.dma("pool", self.WG[s], wg[:, f0:f0 + 256].rearrange("(c p) f -> p c f", p=128),
                 self.s_wgu[s], waits=w)
        t = self.dma("pool", self.WU[s], wu[:, f0:f0 + 256].rearrange("(c p) f -> p c f", p=128),
                     self.s_wgu[s])
        return s, t

    def load_wd(self, wd, c0, n, dcol0):
        s = self.wd_i % 2
        self.wd_i += 1
        t = self.dma("pool", self.WD[s][:, 0:n, :],
                     wd[c0 * 128:(c0 + n) * 128, dcol0:dcol0 + 512].rearrange("(c p) d -> p c d", p=128),
                     self.s_wd[s], waits=[self.wd_free[s]])
        return s, t

    def ffn_tile(self, gain, wg, wu, wd, x_ready, pre_stats=None):
        X, H, AT = self.X, self.H, self.AT
        pend_wgu = [self.load_wgu(wg, wu, 0), self.load_wgu(wg, wu, 256)]
        next_f = 512
        pend_wd = [self.load_wd(wd, FGROUPS[0][0], FGROUPS[0][1], 0),
                   self.load_wd(wd, FGROUPS[0][0], FGROUPS[0][1], 512)]
        h_ready = self.norm_tile(gain, x_ready, pre_stats=pre_stats)
        last_x = None
        for gidx, (c0, n) in enumerate(FGROUPS):
            at_done = None
            for j in range(n):
                fc = c0 + j
                if fc % 2 == 0:
                    slot, wtok = pend_wgu.pop(0)
                for tb in range(2):
                    sl = slice(tb * 512, (tb + 1) * 512)
                    k, pp, pfree = self.get_pair()
                    t_pe = None
                    for wi, W in enumerate((self.WG, self.WU)):
                        for d in range(16):
                            t_pe = self.op("pe", lambda e, W=W, slot=slot, d=d, fc=fc, sl=sl, wi=wi, pp=pp: e.matmul(
                                pp[:, wi, :], W[slot][:, d, (fc % 2) * 128:(fc % 2) * 128 + 128], H[:, d, sl],
                                start=(d == 0), stop=(d == 15)),
                                waits=[wtok, h_ready, pfree] if (d == 0 and wi == 0) else (),
                                post=(d == 15 and wi == 1))
                    s = (2 * j + tb) % 2
                    t_act = self.op("act", lambda e, pp=pp, s=s: e.activation(
                        out=self.SG[s], in_=pp[:, 0, :], func=AF.Silu), waits=[t_pe, self.sg_free[s]])
                    t_dve = self.op("dve", lambda e, pp=pp, s=s, j=j, sl=sl: e.tensor_tensor(
                        out=AT[:, j, sl], in0=self.SG[s], in1=pp[:, 1, :], op=ALU.mult), waits=[t_act])
                    self.sg_free[s] = t_dve
                    self.pair_free[k] = t_dve
                    at_done = t_dve
                if fc % 2 == 1:
                    self.wgu_free[slot] = t_pe
                    if next_f < DFF:
                        pend_wgu.append(self.load_wgu(wg, wu, next_f))
                        next_f += 256
            for q4 in range(4):
                wslot, wdtok = pend_wd.pop(0)
                for dq in range(4):
                    dc = q4 * 4 + dq
                    k, pp, pfree = self.get_pair()
                    t_pe = None
                    for tb in range(2):
                        sl = slice(tb * 512, (tb + 1) * 512)
                        for j in range(n):
                            t_pe = self.op("pe", lambda e, pp=pp, tb=tb, wslot=wslot, j=j, dq=dq, sl=sl, n=n: e.matmul(
                                pp[:, tb, :], self.WD[wslot][:, j, dq * 128:(dq + 1) * 128], AT[:, j, sl],
                                start=(j == 0), stop=(j == n - 1)),
                                waits=[wdtok, at_done, pfree] if (j == 0 and tb == 0) else (),
                                post=(j == n - 1 and tb == 1))
                    t_dve = self.op("dve", lambda e, pp=pp, dc=dc: e.scalar_tensor_tensor(
                        out=X[:, dc, :], in0=pp.rearrange("p a b -> p (a b)"), scalar=0.5, in1=X[:, dc, :],
                        op0=ALU.mult, op1=ALU.add), waits=[t_pe])
                    self.pair_free[k] = t_dve
                    last_x = t_dve
                self.wd_free[wslot] = t_pe
                if q4 + 2 < 4:
                    pend_wd.append(self.load_wd(wd, c0, n, (q4 + 2) * 512))
                elif gidx + 1 < len(FGROUPS):
                    nc0, nn = FGROUPS[gidx + 1]
                    pend_wd.append(self.load_wd(wd, nc0, nn, (q4 - 2) * 512))
        assert not pend_wgu and not pend_wd
        return last_x

    def load_pair(self, apA, apB):
        s = self.wgu_i % 2
        self.wgu_i += 1
        self.dma("pool", self.WG[s], apA.rearrange("(c p) f -> p c f", p=128), self.s_wgu[s],
                 waits=[self.wgu_free[s]])
        t = self.dma("pool", self.WU[s], apB.rearrange("(c p) f -> p c f", p=128), self.s_wgu[s])
        return s, t

    def setup_stg(self):
        self.STG = [self.alloc(1024) for _ in range(2)]
        self.s_stg = [self.sem("stg%d" % i) for i in range(2)]
        self.stg_free = [None, None]
        self.stg_i = getattr(self, "stg_i", 0)
        self.vstg_free = [None, None]
        self.vstg_i = getattr(self, "vstg_i", 0)

    def proj(self, R, w, ngroups, r_ready, evac):
        pend = [self.load_pair(w[:, 0:256], w[:, 256:512])]
        t_pe = None
        for gi in range(ngroups):
            if gi + 1 < ngroups:
                c0 = (gi + 1) * 512
                pend.append(self.load_pair(w[:, c0:c0 + 256], w[:, c0 + 256:c0 + 512]))
            slot, wtok = pend.pop(0)
            for q in range(4):
                W = self.WG if q < 2 else self.WU
                k, pp, pfree = self.get_pair()
                for tb in range(2):
                    sl = slice(tb * 512, (tb + 1) * 512)
                    for d in range(16):
                        t_pe = self.op("pe", lambda e, W=W, slot=slot, d=d, q=q, sl=sl, tb=tb, pp=pp: e.matmul(
                            pp[:, tb, :], W[slot][:, d, (q % 2) * 128:(q % 2) * 128 + 128], R[:, d, sl],
                            start=(d == 0), stop=(d == 15)),
                            waits=[wtok, r_ready, pfree] if (d == 0 and tb == 0) else (),
                            post=(d == 15 and tb == 1))
                self.pair_free[k] = evac(gi * 4 + q, pp, t_pe)
            self.wgu_free[slot] = t_pe
        return t_pe

    def evac_to_dram(self, pp, t_pe, dst, bf16):
        s = self.stg_i % 2
        self.stg_i += 1
        eng = "act" if s == 0 else "dve"
        stg = self.STG[s]
        if bf16:
            stg = stg.bitcast(BF16)[:, 0:1024]
        src = pp.rearrange("p a b -> p (a b)")
        if eng == "act":
            t = self.op("act", lambda e, stg=stg, src=src: e.activation(out=stg, in_=src, func=AF.Copy),
                        waits=[t_pe, self.stg_free[s]])
        else:
            t = self.op("dve", lambda e, stg=stg, src=src: e.tensor_copy(out=stg, in_=src),
                        waits=[t_pe, self.stg_free[s]])
        self.stg_free[s] = self.dma("sp", dst, stg, self.s_stg[s], waits=[t])
        return t

    def final_norm_tile(self, gain, x_ready, dst, tok0):
        X, ACC, RSTD = self.X, self.ACC, self.RSTD
        t_r = self.norm_stats(x_ready)
        last = None
        for c in range(16):
            s = self.stg_i % 2
            self.stg_i += 1
            stg = self.STG[s]
            t = self.op("dve", lambda e, c=c, stg=stg: e.scalar_tensor_tensor(
                out=stg, in0=X[:, c, :], scalar=gain[:, c:c + 1],
                in1=RSTD.rearrange("p a b -> p (a b)"), op0=ALU.mult, op1=ALU.mult),
                waits=[t_r, self.stg_free[s]])
            self.stg_free[s] = self.dma("sp", dst[c * 128:(c + 1) * 128, tok0:tok0 + TT], stg,
                                        self.s_stg[s], waits=[t])
            last = t
        return last

    def norm_stats(self, x_ready):
        X, ACC, RSTD = self.X, self.ACC, self.RSTD
        last_acc = None
        for tb in range(2):
            sl = slice(tb * 512, (tb + 1) * 512)
            for c in range(16):
                if c == 0:
                    last = self.op("act", lambda e, c=c, sl=sl, tb=tb: e.activation(
                        out=ACC[:, tb, :], in_=X[:, c, sl], func=AF.Square),
                        waits=[x_ready, self.acc_free])
                else:
                    s = c % 2
                    t_sq = self.op("act", lambda e, c=c, sl=sl, s=s: e.activation(
                        out=self.SQ[s], in_=X[:, c, sl], func=AF.Square),
                        waits=[x_ready, self.sq_free[s]])
                    last = self.op("dve", lambda e, tb=tb, s=s: e.tensor_tensor(
                        out=ACC[:, tb, :], in0=self.SQ[s], in1=ACC[:, tb, :], op=ALU.add),
                        waits=[t_sq, last])
                    self.sq_free[s] = last
            last_acc = last
        k, pp, pfree = self.get_pair()
        t_pe = None
        for tb in range(2):
            t_pe = self.op("pe", lambda e, tb=tb, pp=pp: e.matmul(
                pp[:, tb, :], self.ones, ACC[:, tb, :], start=True, stop=True),
                waits=[last_acc, pfree, self.tok_ones])
        self.acc_free = t_pe
        t_act = self.op("act", lambda e, pp=pp: e.activation(
            out=RSTD, in_=pp, func=AF.Sqrt, bias=self.eps_ap, scale=1.0 / D),
            waits=[t_pe, self.rstd_free])
        self.pair_free[k] = t_act
        t_r = self.op("dve", lambda e: e.reciprocal(out=RSTD, in_=RSTD), waits=[t_act])
        return t_r

    def norm_tile(self, gain, x_ready, pre_stats=None):
        X, H, RSTD = self.X, self.H, self.RSTD
        t_r = pre_stats if pre_stats is not None else self.norm_stats(x_ready)
        t_h = None
        for c in range(16):
            t_h = self.op("dve", lambda e, c=c: e.scalar_tensor_tensor(
                out=H[:, c, :], in0=X[:, c, :], scalar=gain[:, c:c + 1],
                in1=RSTD.rearrange("p a b -> p (a b)"), op0=ALU.mult, op1=ALU.mult),
                waits=[t_r, x_ready, self.h_free])
        self.rstd_free = t_h
        return t_h

    def inproj_tile(self, w_in, h_ready, qT, kT, V, zr, tok0, need_q=True, need_zr=True, mid_cb=None):
        H = self.H

        def evac(chunk, pp, t_pe, base):
            zc = base + chunk
            if zc < 8:
                dst, bf = qT[zc * 128:(zc + 1) * 128, tok0:tok0 + TT], True
            elif zc < 16:
                dst, bf = kT[(zc - 8) * 128:(zc - 7) * 128, tok0:tok0 + TT], True
            else:
                dst, bf = zr[(zc - 24) * 128:(zc - 23) * 128, tok0:tok0 + TT], False
            return self.evac_to_dram(pp, t_pe, dst, bf)

        if need_q:
            t1 = self.proj(H, w_in[:, 0:2048], 4, h_ready, lambda c, pp, t: evac(c, pp, t, 0))
        else:
            t1 = self.proj(H, w_in[:, 1024:2048], 2, h_ready, lambda c, pp, t: evac(c, pp, t, 8))
        t_pe = None
        for vi in range(2):
            c0 = 2048 + vi * 512
            slot, wtok = self.load_pair(w_in[:, c0:c0 + 256], w_in[:, c0 + 256:c0 + 512])
            for tkb in range(TT // 128):
                k, pp, pfree = self.get_pair()
                for half, W in enumerate((self.WG, self.WU)):
                    for d in range(16):
                        t_pe = self.op("pe", lambda e, W=W, slot=slot, d=d, tkb=tkb, half=half, pp=pp: e.matmul(
                            pp[:, half, 0:256], H[:, d, tkb * 128:(tkb + 1) * 128], W[slot][:, d, :],
                            start=(d == 0), stop=(d == 15)),
                            waits=[wtok, h_ready, pfree] if (d == 0 and half == 0) else (),
                            post=(d == 15 and half == 1))
                s = self.vstg_i % 2
                self.vstg_i += 1
                stg = self.VSTG[s]
                t = None
                for half in range(2):
                    o_ = stg[:, 4 * half:4 * half + 4, 0:64]
                    i_ = pp[:, half, 0:256].rearrange("p (h c) -> p h c", c=64)
                    if half == 0:
                        t = self.op("act", lambda e, o_=o_, i_=i_: e.activation(out=o_, in_=i_, func=AF.Copy),
                                    waits=[t_pe, self.vstg_free[s], self.tok_vstg])
                    else:
                        t = self.op("dve", lambda e, o_=o_, i_=i_: e.tensor_copy(out=o_, in_=i_),
                                    waits=[t_pe, self.vstg_free[s], self.tok_vstg, t])
                dst = V[tok0 + tkb * 128:tok0 + (tkb + 1) * 128, vi * 520:(vi + 1) * 520]
                self.vstg_free[s] = self.dma("sp", dst, stg.rearrange("p h c -> p (h c)"), self.sem("vstg%d" % s),
                                             waits=[t])
                self.pair_free[k] = t
            self.wgu_free[slot] = t_pe
        if mid_cb is not None:
            mid_cb()
        if not need_zr:
            return t_pe
        t3 = self.proj(H, w_in[:, 3072:5120], 4, h_ready, lambda c, pp, t: evac(c, pp, t, 24))
        return t3

    def convpool_stage(self, zr, zrh, CW, PSC, PW, IC16, yT):
        E = 16
        prev = [None]

        def dv(fn, waits=()):
            t = self.op("dve", fn, waits=list(waits) + [prev[0]])
            prev[0] = t
            return t
        GB = self.alloc_f32(1, NT)[:, 0, :]
        GC = self.alloc(NT + E)
        CI = self.alloc(NT + E)
        T1 = self.alloc(NT)
        YC = [self.alloc(NT // 2).bitcast(BF16) for _ in range(2)]
        SB = self.alloc(NT + E)
        s_ld = self.sem("cp_ld")
        s_st = [self.sem("cp_st0"), self.sem("cp_st1")]
        yc_free = [None, None]
        yi = 0
        free = None
        for cc in range(4):
            self.dma("sp", GB, zr[cc * 128:(cc + 1) * 128, :], s_ld, waits=[free])
            self.dma("sp", GC[:, E:], zr[(4 + cc) * 128:(5 + cc) * 128, :], s_ld)
            self.dma("sp", GC[:, 0:E], zrh[cc * 128:(cc + 1) * 128, :], s_ld)
            self.dma("sp", CI[:, E:], zr[(8 + cc) * 128:(9 + cc) * 128, :], s_ld)
            ld = self.dma("sp", CI[:, 0:E], zrh[(4 + cc) * 128:(5 + cc) * 128, :], s_ld)
            dv(lambda e: e.tensor_tensor(out=GC, in0=GC, in1=CI, op=ALU.mult), waits=[ld])
            yield
            dv(lambda e, cc=cc: e.tensor_scalar(
                out=T1, in0=GC[:, E - 2:E - 2 + NT], scalar1=CW[:, cc, 0:1], scalar2=None, op0=ALU.mult))
            yield
            dv(lambda e, cc=cc: e.scalar_tensor_tensor(
                out=T1, in0=GC[:, E - 1:E - 1 + NT], scalar=CW[:, cc, 1:2], in1=T1, op0=ALU.mult, op1=ALU.add))
            yield
            dv(lambda e, cc=cc: e.scalar_tensor_tensor(
                out=T1, in0=GC[:, E:E + NT], scalar=CW[:, cc, 2:3], in1=T1, op0=ALU.mult, op1=ALU.add))
            yield
            s = yi % 2
            yi += 1
            t = dv(lambda e, s=s: e.tensor_tensor(out=YC[s], in0=T1, in1=GB, op=ALU.mult),
                   waits=[yc_free[s]])
            free = t
            yc_free[s] = self.dma("sp", yT[(8 + cc) * 128:(9 + cc) * 128, :], YC[s], s_st[s], waits=[t])
            yield
        PIN = GC
        SA = CI
        PB = T1.bitcast(BF16)[:, 0:NT]
        for g in range(4):
            w = 2 << g
            self.dma("sp", PIN[:, E:], zr[(12 + g) * 128:(13 + g) * 128, :], s_ld, waits=[free])
            ld = self.dma("sp", PIN[:, 0:E], zrh[(8 + g) * 128:(9 + g) * 128, :], s_ld)
            cur, bufs, look = PIN, [SA, SB], 0
            first = True
            for k in range(g + 1):
                step = 1 << k
                nl = look + step
                nxt = bufs[k % 2]
                dv(lambda e, cur=cur, nxt=nxt, nl=nl, look=look, step=step: e.tensor_tensor(
                    out=nxt[:, nl:], in0=cur[:, nl:], in1=cur[:, look:NT + E - step], op=ALU.add),
                    waits=[ld] if first else ())
                yield
                first = False
                cur, look = nxt, nl
            tmp = bufs[(g + 1) % 2]
            dv(lambda e, cur=cur, tmp=tmp, w=w: e.scalar_tensor_tensor(
                out=tmp[:, E:], in0=cur[:, E:], scalar=1.0 / w, in1=PIN[:, E:], op0=ALU.mult, op1=ALU.subtract))
            yield
            dv(lambda e, cur=cur, g=g: e.tensor_tensor(
                out=cur[:, E:2 * E], in0=cur[:, E:2 * E], in1=IC16[:, g, :], op=ALU.mult))
            dv(lambda e, cur=cur, tmp=tmp: e.tensor_tensor(
                out=tmp[:, E:2 * E], in0=cur[:, E:2 * E], in1=PIN[:, E:2 * E], op=ALU.subtract))
            t_pb = dv(lambda e, tmp=tmp: e.tensor_copy(out=PB, in_=tmp[:, E:]))
            free = t_pb
            yield
            s = yi % 2
            yi += 1
            t = None
            for qtr in range(4):
                k, pb, pfree = self.get_bank()
                c0_ = qtr * 512
                t_pe = self.op("pe", lambda e, g=g, pb=pb, c0_=c0_: e.matmul(
                    pb, PW[:, g, :], PB[:, c0_:c0_ + 512], start=True, stop=True),
                    waits=[t_pb, pfree, self.tok_pw])
                t = dv(lambda e, g=g, pb=pb, s=s, c0_=c0_: e.tensor_scalar(
                    out=YC[s][:, c0_:c0_ + 512], in0=pb,
                    scalar1=PSC[:, g:g + 1], scalar2=None, op0=ALU.mult), waits=[t_pe, yc_free[s]])
                self.bank_free[k] = t
                free = t_pe
                yield
            yc_free[s] = self.dma("sp", yT[(12 + g) * 128:(13 + g) * 128, :], YC[s], s_st[s], waits=[t])

    def get_bank(self):
        b_ = self.nbank % 8
        self.nbank += 1
        return b_, self.ps[:, b_, :], self.bank_free[b_]

    def attention_stage(self, qT, kT, kTh, V, Vh, MASK, yT, bg=None):
        QT = [self.alloc_bf16(1, NT)[:, 0, :] for _ in range(2)]
        KT = [self.alloc_bf16(1, 2 * NT)[:, 0, :] for _ in range(2)]
        V1 = [self.alloc_bf16(32, 130) for _ in range(2)]
        V4 = [self.alloc_bf16(32, 130) for _ in range(2)]
        V16 = [self.alloc_bf16(32, 130) for _ in range(2)]
        ACC = [self.alloc(NT) for _ in range(2)]
        SEL = self.alloc(64)
        self.op("dve", lambda e: e.memset(SEL[0:64, :], 0.0))
        t_sel = self.op("dve", lambda e: e.memset(SEL[64:65, :], 1.0))
        YH = [self.alloc_bf16(1, NT)[:, 0, :] for _ in range(2)]
        ET = [self.alloc_bf16(1, 256)[:, 0, :] for _ in range(8)]
        s_ld = [self.sem("at_ld0"), self.sem("at_ld1")]
        s_yh = [self.sem("at_yh0"), self.sem("at_yh1")]
        att_free = [None, None]
        acc_free = [None, None]
        yh_free = [None, None]
        et_free = [None] * 8
        bank_free = self.bank_free
        get_bank = self.get_bank
        nunit = [0]

        def bg_step():
            if bg is not None:
                next(bg, None)

        def issue_loads(hc):
            s = hc % 2
            rs_ = slice(hc * 128, (hc + 1) * 128)
            cs = slice(hc * 130, (hc + 1) * 130)
            w = [att_free[s]]
            self.dma("sp", QT[s], qT[rs_, :], s_ld[s], waits=w)
            self.dma("sp", KT[s][:, 0:NT], kTh[rs_, :], s_ld[s])
            self.dma("sp", KT[s][:, NT:], kT[rs_, :], s_ld[s])
            for kb, src in enumerate((Vh, V)):
                self.dma("sp", V1[s][:, 16 * kb:16 * kb + 16, :],
                         src[:, cs].rearrange("(b p) c -> p b c", p=128), s_ld[s])
                self.dma("sp", V16[s][:, 16 * kb:16 * kb + 16, :],
                         src[:, cs].rearrange("(i r) c -> i r c", r=16), s_ld[s])
                v4 = src[:, cs].rearrange("(kb i r) c -> r i kb c", i=128, r=4)
                for r in range(4):
                    t = self.dma("sp", V4[s][:, r * 8 + 4 * kb:r * 8 + 4 * kb + 4, :], v4[r], s_ld[s])
            return t

        ld_tok = {0: issue_loads(0)}
        hidx = 0
        ei = 0
        for hc in range(8):
            s = hc % 2
            if hc + 1 < 8:
                ld_tok[hc + 1] = issue_loads(hc + 1)
            ld = ld_tok[hc]
            last_pe = None
            for hd in range(2):
                a = hidx % 2
                hidx += 1
                ps_ = slice(hd * 64, (hd + 1) * 64)
                units = []
                for (st, Vb) in ((1, V1[s]), (4, V4[s]), (16, V16[s])):
                    nblk = 16 // st
                    for r in range(st):
                        for b in range(nblk):
                            Bq = nblk + b
                            qs = slice(r + st * 128 * b, r + st * 128 * b + st * 127 + 1, st)
                            kts, vts = [], []
                            for kb in (Bq - 1, Bq):
                                kts.append(slice(r + st * 128 * kb, r + st * 128 * kb + st * 127 + 1, st))
                                if st == 1:
                                    vts.append(kb)
                                elif st == 4:
                                    vts.append(r * 8 + kb)
                                else:
                                    vts.append(kb * 16 + r)
                            units.append((st, qs, kts, vts, Vb, b == 0))
                state = {}

                def emit_S(n):
                    nonlocal ei
                    st, qs, kts, vts, Vb, first = units[n]
                    k, pp, pfree = get_bank()
                    t_s = None
                    m = 1 if first else 0
                    for kbi in range(2):
                        t_s = self.op("pe", lambda e, pp=pp, kbi=kbi, kts=kts, qs=qs, s=s, ps_=ps_: e.matmul(
                            pp[:, kbi * 128:(kbi + 1) * 128], KT[s][ps_, kts[kbi]], QT[s][ps_, qs],
                            start=True, stop=True), waits=[ld, pfree] if kbi == 0 else (), post=(kbi == 1))
                    ex = ei % 8
                    ei += 1
                    t_e = self.op("act", lambda e, pp=pp, ex=ex: e.activation(
                        out=ET[ex], in_=pp[:, 0:256], func=AF.Exp, scale=0.125),
                        waits=[t_s, et_free[ex]])
                    meng = "dve" if n % 3 == 2 else "pool"
                    t_m = self.op(meng, lambda e, ex=ex, m=m: e.tensor_tensor(
                        out=ET[ex], in0=ET[ex], in1=MASK[:, m, :], op=ALU.mult), waits=[t_e, self.tok_mask])
                    state[n] = (k, pp, ex, t_m)

                def emit_NL(n):
                    st, qs, kts, vts, Vb, first = units[n]
                    k, pp, ex, t_m = state.pop(n)
                    t_nl = None
                    for kbi in range(2):
                        t_nl = self.op("pe", lambda e, pp=pp, kbi=kbi, vts=vts, Vb=Vb, ex=ex, hd=hd: e.matmul(
                            pp[0:65, 256:384], Vb[:, vts[kbi], hd * 65:hd * 65 + 65],
                            ET[ex][:, kbi * 128:(kbi + 1) * 128],
                            start=(kbi == 0), stop=(kbi == 1)), waits=[t_m] if kbi == 0 else (), post=(kbi == 1))
                    et_free[ex] = t_nl
                    src = pp[0:65, 256:384]
                    if st == 1:
                        t_acc = self.op("dve", lambda e, src=src, qs=qs, a=a: e.tensor_copy(
                            out=ACC[a][0:65, qs], in_=src), waits=[t_nl, acc_free[a]])
                    else:
                        t_acc = self.op("dve", lambda e, src=src, qs=qs, a=a: e.tensor_tensor(
                            out=ACC[a][0:65, qs], in0=src, in1=ACC[a][0:65, qs], op=ALU.add),
                            waits=[t_nl])
                    bank_free[k] = t_acc
                    return t_nl, t_acc

                LA = 5
                t_acc = None
                for n in range(len(units) + LA):
                    if n < len(units):
                        emit_S(n)
                    if n >= LA:
                        last_pe, t_acc = emit_NL(n - LA)
                    nunit[0] += 1
                    if nunit[0] % 8 == 0:
                        bg_step()
                t_rc = self.op("dve", lambda e, a=a: e.reciprocal(out=ACC[a][64:65, :], in_=ACC[a][64:65, :]),
                               waits=[t_acc])
                t_y = None
                for qtr in range(4):
                    k, pp, pfree = get_bank()
                    c0_ = qtr * 512
                    t_b = self.op("pe", lambda e, pp=pp, a=a, c0_=c0_: e.matmul(
                        pp[0:64, :], SEL[0:65, :], ACC[a][0:65, c0_:c0_ + 512], start=True, stop=True),
                        waits=[t_rc, pfree, t_sel])
                    t_y = self.op("dve", lambda e, a=a, pp=pp, c0_=c0_: e.tensor_tensor(
                        out=YH[a][0:64, c0_:c0_ + 512], in0=ACC[a][0:64, c0_:c0_ + 512],
                        in1=pp[0:64, :], op=ALU.mult),
                        waits=[t_b, yh_free[a]])
                    bank_free[k] = t_y
                acc_free[a] = t_y
                yh_free[a] = self.dma("sp", yT[hc * 128 + hd * 64:hc * 128 + hd * 64 + 64, :], YH[a][0:64, :],
                                      s_yh[a], waits=[t_y])
            att_free[s] = last_pe
        if bg is not None:
            for _ in bg:
                pass

    def load_y_tile(self, yT, tok0):
        t = None
        for i in range(4):
            t = self.dma("sp", self.H[:, 4 * i:4 * i + 4, :],
                         yT[i * 512:(i + 1) * 512, tok0:tok0 + TT].rearrange("(c p) t -> p c t", p=128),
                         self.sem("yld"), waits=[self.h_free] if i == 0 else ())
        return t

    def outproj_tile(self, w_out, x_ready, y_ready):
        X = self.X
        last = [None]

        def evac(chunk, pp, t_pe):
            t = self.op("dve", lambda e, pp=pp, chunk=chunk: e.tensor_tensor(
                out=X[:, chunk, :], in0=pp.rearrange("p a b -> p (a b)"), in1=X[:, chunk, :], op=ALU.add),
                waits=[t_pe, x_ready])
            last[0] = t
            return t
        t_pe = self.proj(self.H, w_out, 4, y_ready, evac)
        self.h_free = t_pe
        return last[0]


def _stage_A(B, regions, xsrc, x1, qT, kT, V, zr, G1, GM, wg, wu, wd, w_in, need_q):
    tiles = [(r, t) for r in regions for t in range(NT // TT)]
    xr = B.load_x(xsrc[tiles[0][0]], tiles[0][1] * TT)
    pre = None
    for i, (r, t) in enumerate(tiles):
        done = B.ffn_tile(G1, wg, wu, wd, xr, pre_stats=pre)
        st = B.store_x(x1[r], t * TT, [done])
        h_ready = B.norm_tile(GM, done)
        B.x_free = [st, h_ready]
        nxt = {}
        if i + 1 < len(tiles):
            nxt["xr"] = B.load_x(xsrc[tiles[i + 1][0]], tiles[i + 1][1] * TT)

            def mid(nxt=nxt):
                nxt["pre"] = B.norm_stats(nxt["xr"])
        else:
            mid = None
        B.h_free = B.inproj_tile(w_in, h_ready, qT[r], kT[r], V[r], zr[r], t * TT, need_q=need_q[r],
                                 need_zr=(need_q[r] or t == NT // TT - 1), mid_cb=mid)
        xr, pre = nxt.get("xr"), nxt.get("pre")
        B.x_free = []


def _stage_B(B, base, r, x1, qT, kT, V, zr, yT, x3, outT, CW, PSC, PW, IC16, MASK, G2, GF,
             w_out, wg, wu, wd, final):
    B.full_barrier()
    B.aoff = base
    zrh = zr[r + 1][512:2048, NT - 16:NT]
    B.bank_free = [None] * 8
    B.nbank = 0
    bg = B.convpool_stage(zr[r], zrh, CW, PSC, PW, IC16, yT)
    B.attention_stage(qT[r], kT[r], kT[r + 1], V[r], V[r + 1], MASK, yT, bg=bg)
    B.full_barrier()
    B.aoff = base
    B.setup_ffn_bufs()
    B.setup_stg()
    for t in range(NT // TT):
        xr = B.load_x(x1[r], t * TT)
        yr = B.load_y_tile(yT, t * TT)
        x2 = B.outproj_tile(w_out, xr, yr)
        done = B.ffn_tile(G2, wg, wu, wd, x2)
        B.h_free = done
        if final:
            fin = B.final_norm_tile(GF, done, outT, t * TT)
            B.x_free = [fin]
        else:
            st = B.store_x(x3[r], t * TT, [done])
            B.x_free = [st]


NCONST = 64 + 24 + 8 + 128 + 16


def build_fused():
    nc = bass.Bass("TRN2", target_bir_lowering=False)
    xin = nc.dram_tensor("xin", [3, D, NT], F32, kind="ExternalInput").ap()
    consts = nc.dram_tensor("consts", [128, NCONST + 32], F32, kind="ExternalInput").ap()
    masks = nc.dram_tensor("masks", [128, 4, 256], BF16, kind="ExternalInput").ap()
    W = []
    for l in range(2):
        d = {}
        for nm, shp in (("wg1", [D, DFF]), ("wu1", [D, DFF]), ("wd1", [DFF, D]), ("w_in", [D, DIN]),
                        ("w_out", [D, D]), ("wg2", [D, DFF]), ("wu2", [D, DFF]), ("wd2", [DFF, D]),
                        ("pool_w", [4, 128, 128])):
            d[nm] = nc.dram_tensor("%s_%d" % (nm, l), shp, F32, kind="ExternalInput").ap()
        W.append(d)
    outT = nc.dram_tensor("outT", [D, NT], F32, kind="ExternalOutput").ap()
    x1 = [nc.dram_tensor("x1_%d" % r, [D, NT], F32).ap() for r in range(3)]
    x3 = [nc.dram_tensor("x3_%d" % r, [D, NT], F32).ap() for r in range(2)]
    qT = [nc.dram_tensor("q_%d" % r, [1024, NT], BF16).ap() for r in range(3)]
    kT = [nc.dram_tensor("k_%d" % r, [1024, NT], BF16).ap() for r in range(3)]
    V = [nc.dram_tensor("v_%d" % r, [NT, 1040], BF16).ap() for r in range(3)]
    zr = [nc.dram_tensor("zr_%d" % r, [2048, NT], F32).ap() for r in range(3)]
    yT = nc.dram_tensor("yT", [D, NT], BF16).ap()
    xsrc = [xin[r] for r in range(3)]
    with ExitStack() as stack:
        B = Builder(nc, stack)
        B.x_free = []
        B.acc_free = None
        B.rstd_free = None
        B.h_free = None
        B.eps_ap = B.alloc(1)
        t_eps = B.op("dve", lambda e: e.memset(B.eps_ap, EPS))
        B.wait_only("act", [t_eps])
        C = B.alloc(NCONST + 32)
        MASK = B.alloc_bf16(4, 256)
        PW = [B.alloc_bf16(4, 128) for _ in range(2)]
        s_c = B.sem("consts")
        B.dma("sp", C, consts[:, :], s_c)
        B.tok_mask = B.dma("sp", MASK, masks[:, :, :], s_c)
        B.wait_only("dve", [B.tok_mask])
        s_pw = B.sem("pw")
        for l in range(2):
            B.tok_pw = B.dma("pool", PW[l], W[l]["pool_w"].rearrange("g c d -> c g d"), s_pw)
        G = lambda i: C[:, 16 * i:16 * i + 16]
        CWs = [C[:, 112 + 12 * l:112 + 12 * l + 12].rearrange("p (a b) -> p a b", a=4) for l in range(2)]
        PSCs = [C[:, 136 + 4 * l:136 + 4 * l + 4] for l in range(2)]
        IC = [C[:, 144 + 64 * k:144 + 64 * k + 64].rearrange("p (a b) -> p a b", a=4) for k in range(2)]
        MK = [MASK[:, 2 * k:2 * k + 2, :] for k in range(2)]
        base = B.aoff
        B.setup_ffn_bufs()
        B.setup_stg()
        _stage_A(B, [2, 1, 0], xsrc, x1, qT, kT, V, zr, G(0), G(1), W[0]["wg1"], W[0]["wu1"], W[0]["wd1"],
                 W[0]["w_in"], need_q={2: False, 1: True, 0: True})
        for r in (1, 0):
            _stage_B(B, base, r, x1, qT, kT, V, zr, yT, x3, None, CWs[0], PSCs[0], PW[0], IC[r], MK[r],
                     G(2), None, W[0]["w_out"], W[0]["wg2"], W[0]["wu2"], W[0]["wd2"], final=False)
        B.full_barrier()
        _stage_A(B, [1, 0], x3, x1, qT, kT, V, zr, G(3), G(4), W[1]["wg1"], W[1]["wu1"], W[1]["wd1"],
                 W[1]["w_in"], need_q={1: False, 0: True})
        _stage_B(B, base, 0, x1, qT, kT, V, zr, yT, x3, outT, CWs[1], PSCs[1], PW[1], IC[0], MK[0],
                 G(5), G(6), W[1]["w_out"], W[1]["wg2"], W[1]["wu2"], W[1]["wd2"], final=True)
        B.full_barrier()
        B.emit()
    return nc


_CACHE = {}


def _chunked(v):
    return np.ascontiguousarray(np.asarray(v, np.float32).reshape(-1, 128).T)


def kernel(x, ffn1_norm, ffn1_w_gate, ffn1_w_up, ffn1_w_down, mix_norm, w_in, conv_w,
           pool_w, pool_scale, w_out, ffn2_norm, ffn2_w_gate, ffn2_w_up, ffn2_w_down, final_norm):
    f = lambda a: np.ascontiguousarray(np.asarray(a, np.float32))
    x = f(x)
    cores = list(range(NCORES))
    if "nc" not in _CACHE:
        _CACHE["nc"] = build_fused()
    nc = _CACHE["nc"]
    jj = np.arange(128)[:, None]
    ii = np.arange(128)[None, :]
    m_prev = (jj >= ii).astype(np.float32)
    m_cur = (jj <= ii).astype(np.float32)
    wts = {}
    for l in range(2):
        wts.update({"wg1_%d" % l: f(ffn1_w_gate[l]), "wu1_%d" % l: f(ffn1_w_up[l]), "wd1_%d" % l: f(ffn1_w_down[l]),
                    "w_in_%d" % l: f(w_in[l]), "w_out_%d" % l: f(w_out[l]), "wg2_%d" % l: f(ffn2_w_gate[l]),
                    "wu2_%d" % l: f(ffn2_w_up[l]), "wd2_%d" % l: f(ffn2_w_down[l]), "pool_w_%d" % l: f(pool_w[l])})
    gains = [ffn1_norm[0], mix_norm[0], ffn2_norm[0], ffn1_norm[1], mix_norm[1], ffn2_norm[1], final_norm]
    cws = []
    for l in range(2):
        cw = np.asarray(conv_w[l], np.float32)
        cws.append(np.ascontiguousarray(cw.T.reshape(4, 128, 3).transpose(1, 0, 2)).reshape(128, 12))
    ins = []
    for c in cores:
        b, p = c // 4, c % 4
        xin = np.zeros((3, D, NT), np.float32)
        for r in range(3):
            if p - r >= 0:
                xin[r] = x[b, (p - r) * NT:(p - r + 1) * NT, :].T
        mk = np.zeros((128, 4, 256), np.float32)
        ics = []
        for r in range(2):
            halo_virtual = (p - r - 1) < 0
            mk[:, 2 * r, 0:128] = m_prev
            mk[:, 2 * r, 128:256] = m_cur
            mk[:, 2 * r + 1, 0:128] = 0.0 if halo_virtual else m_prev
            mk[:, 2 * r + 1, 128:256] = m_cur
            pos = max(p - r, 0) * NT + np.arange(16)
            ics.append(np.stack([1.0 / np.minimum(pos + 1, w) for w in (2, 4, 8, 16)]).astype(np.float32).reshape(1, 64))
        cst = np.concatenate(
            [_chunked(g) for g in gains] + cws + [_chunked(pool_scale[0]), _chunked(pool_scale[1])]
            + [np.broadcast_to(ic, (128, 64)) for ic in ics], axis=1)
        assert cst.shape == (128, NCONST + 32), cst.shape
        ins.append(dict(xin=xin, consts=np.ascontiguousarray(cst.astype(np.float32)),
                        masks=mk.astype(ml_dtypes.bfloat16), **wts))
    res = run_bass_kernel_spmd(nc, ins, core_ids=cores).results
    y = np.empty((2, 4 * NT, D), np.float32)
    for c in cores:
        y[c // 4, (c % 4) * NT:(c % 4 + 1) * NT, :] = res[c]["outT"].T
    return y
```

```python
import numpy as np
import ml_dtypes
from contextlib import ExitStack
import concourse.bass as bass
import concourse.mybir as mybir
from concourse.bass_utils import run_bass_kernel_spmd

F32 = mybir.dt.float32
BF16 = mybir.dt.bfloat16
AF = mybir.ActivationFunctionType
ALU = mybir.AluOpType

D = 2048
NT = 2048
TT = 1024
DFF = 5632
DIN = 5120
NCORES = 8
EPS = 1e-6
ENGS = ["pe", "act", "dve", "pool", "sp"]
FGROUPS = [(0, 12), (12, 12), (24, 12), (36, 8)]


class Sem:
    def __init__(self, h, name):
        self.h, self.n, self.name = h, 0, name


class Builder:
    def __init__(self, nc, stack):
        self.nc, self.stack = nc, stack
        self.streams = {e: [] for e in ENGS}
        self.cnt = {e: self.sem("c_" + e) for e in ["pe", "act", "dve", "pool"]}
        self.arena_words = 51 * 1024 + 512
        self.arena = stack.enter_context(nc.sbuf_tensor("arena", [128, self.arena_words], F32))
        self.aoff = 0
        self.ps = stack.enter_context(nc.psum_tensor("ps", [128, 8, 512], F32))
        self.pair_free = [None] * 4
        self.pair_i = 0
        self.nsem = 0

    def sem(self, name):
        if not hasattr(self, "_sems"):
            self._sems = {}
        if name not in self._sems:
            self._sems[name] = Sem(self.stack.enter_context(self.nc.semaphore(name)), name)
        return self._sems[name]

    def full_barrier(self):
        toks = [(s_, s_.n) for s_ in self._sems.values() if s_.n > 0]
        for e in ENGS:
            self.wait_only(e, toks)
        self.pair_free = [None] * 4
        for nm in ("wgu_free", "wd_free", "sg_free", "stg_free", "vstg_free"):
            if hasattr(self, nm):
                setattr(self, nm, [None, None])
        if hasattr(self, "sg_free"):
            self.sq_free = self.sg_free
        self.x_free = []
        self.acc_free = None
        self.rstd_free = None
        self.h_free = None

    def alloc(self, words):
        off = self.aoff
        self.aoff += words
        assert self.aoff <= self.arena_words, (self.aoff, self.arena_words)
        return self.arena[:, off:off + words]

    def alloc_f32(self, *shape):
        n = int(np.prod(shape))
        ap = self.alloc(n)
        if len(shape) == 2:
            return ap.rearrange("p (a b) -> p a b", a=shape[0])
        if len(shape) == 3:
            return ap.rearrange("p (a b c) -> p a b c", a=shape[0], b=shape[1])
        return ap

    def alloc_bf16(self, *shape):
        n = int(np.prod(shape))
        assert n % 2 == 0
        ap = self.alloc(n // 2).bitcast(BF16)
        if len(shape) == 2:
            return ap.rearrange("p (a b) -> p a b", a=shape[0])
        if len(shape) == 3:
            return ap.rearrange("p (a b c) -> p a b c", a=shape[0], b=shape[1])
        return ap

    def op(self, eng, fn, waits=(), post=True):
        sem = None
        tok = None
        if post:
            sem = self.cnt[eng]
            sem.n += 1
            tok = (sem, sem.n)
        self.streams[eng].append((fn, [w for w in waits if w is not None], sem, 1))
        return tok

    def dma(self, eng, out, in_, sem, waits=()):
        sem.n += 16
        self.streams[eng].append(
            (lambda e, out=out, in_=in_: e.dma_start(out=out, in_=in_),
             [w for w in waits if w is not None], sem, 16))
        return (sem, sem.n)

    def wait_only(self, eng, waits):
        self.streams[eng].append((None, [w for w in waits if w is not None], None, 0))

    def get_pair(self):
        k = self.pair_i % 4
        self.pair_i += 1
        return k, self.ps[:, 2 * k:2 * k + 2, :], self.pair_free[k]

    def barrier(self, extra=()):
        toks = [(s, s.n) for s in self.cnt.values() if s.n > 0] + list(extra)
        for e in ENGS:
            self.wait_only(e, toks)

    def emit(self):
        nc = self.nc
        with nc.Block() as block:
            def mk(name):
                items = self.streams[name]

                def run(e):
                    seen = {}
                    for fn, waits, sem, inc in items:
                        for (s, v) in waits:
                            if seen.get(s.name, 0) < v:
                                e.wait_ge(s.h, v)
                                seen[s.name] = v
                        if fn is not None:
                            ins = fn(e)
                            if sem is not None:
                                ins.then_inc(sem.h, inc)
                return run
            block.tensor(mk("pe"))
            block.scalar(mk("act"))
            block.vector(mk("dve"))
            block.gpsimd(mk("pool"))
            block.sync(mk("sp"))

    def setup_ffn_bufs(self):
        self.X = self.alloc_f32(16, TT)
        self.H = self.alloc_bf16(16, TT)
        self.AT = self.alloc_bf16(12, TT)
        self.WG = [self.alloc_bf16(16, 256) for _ in range(2)]
        self.WU = [self.alloc_bf16(16, 256) for _ in range(2)]
        self.WD = [self.alloc_bf16(12, 512) for _ in range(2)]
        self.ACC = self.alloc_f32(2, 512)
        self.RSTD = self.alloc_f32(2, 512)
        self.SG = [self.alloc(512) for _ in range(2)]
        self.SQ = self.SG
        self.VSTG = [self.alloc(260).bitcast(BF16).rearrange("p (h c) -> p h c", c=65) for _ in range(2)]
        self.tok_vstg = None
        for i_ in range(2):
            self.tok_vstg = self.op("dve", lambda e, i_=i_: e.memset(self.VSTG[i_], 1.0))
        self.ones = self.alloc(128)
        self.s_wgu = [self.sem("wgu%d" % i) for i in range(2)]
        self.s_wd = [self.sem("wd%d" % i) for i in range(2)]
        self.s_xld = self.sem("xld")
        self.s_xst = self.sem("xst")
        self.wgu_free = [None, None]
        self.wd_free = [None, None]
        self.wgu_i = getattr(self, "wgu_i", 0)
        self.wd_i = getattr(self, "wd_i", 0)
        self.sg_free = [None, None]
        self.sq_free = self.sg_free
        self.x_free = []
        self.tok_ones = self.op("dve", lambda e: e.memset(self.ones, 1.0))

    def load_x(self, src, tok0):
        toks = []
        for i in range(4):
            t = self.dma("sp", self.X[:, 4 * i:4 * i + 4, :],
                         src[i * 512:(i + 1) * 512, tok0:tok0 + TT].rearrange("(c p) t -> p c t", p=128),
                         self.s_xld, waits=list(self.x_free) if i == 0 else ())
            toks.append(t)
        return toks[-1]

    def store_x(self, dst, tok0, waits):
        t = None
        for i in range(4):
            t = self.dma("sp", dst[i * 512:(i + 1) * 512, tok0:tok0 + TT].rearrange("(c p) t -> p c t", p=128),
                         self.X[:, 4 * i:4 * i + 4, :], self.s_xst, waits=waits if i == 0 else ())
        return t

    def norm_tile_old(self, gain, x_ready):
        X, H, ACC, RSTD = self.X, self.H, self.ACC, self.RSTD
        last_acc = None
        for tb in range(2):
            sl = slice(tb * 512, (tb + 1) * 512)
            for c in range(16):
                if c == 0:
                    t_sq = self.op("act", lambda e, c=c, sl=sl, tb=tb: e.activation(
                        out=ACC[:, tb, :], in_=X[:, c, sl], func=AF.Square),
                        waits=[x_ready, last_acc if tb == 0 else None])
                    last = t_sq
                else:
                    s = c % 2
                    t_sq = self.op("act", lambda e, c=c, sl=sl, s=s: e.activation(
                        out=self.SQ[s], in_=X[:, c, sl], func=AF.Square),
                        waits=[x_ready, self.sq_free[s]])
                    last = self.op("dve", lambda e, tb=tb, s=s: e.tensor_tensor(
                        out=ACC[:, tb, :], in0=self.SQ[s], in1=ACC[:, tb, :], op=ALU.add),
                        waits=[t_sq, last])
                    self.sq_free[s] = last
            last_acc = last
        k, pp, pfree = self.get_pair()
        t_pe = None
        for tb in range(2):
            t_pe = self.op("pe", lambda e, tb=tb, pp=pp: e.matmul(
                pp[:, tb, :], self.ones.rearrange("p (a b) -> p a b", a=1)[:, 0, :], ACC[:, tb, :],
                start=True, stop=True), waits=[last_acc, pfree, self.tok_ones])
        t_act = self.op("act", lambda e, pp=pp: e.activation(
            out=RSTD, in_=pp, func=AF.Sqrt, bias=self.eps_ap, scale=1.0 / D), waits=[t_pe])
        self.pair_free[k] = t_act
        t_r = self.op("dve", lambda e: e.reciprocal(out=RSTD, in_=RSTD), waits=[t_act])
        t_h = None
        for c in range(16):
            t_h = self.op("dve", lambda e, c=c: e.scalar_tensor_tensor(
                out=H[:, c, :], in0=X[:, c, :], scalar=gain[:, c:c + 1],
                in1=RSTD.rearrange("p a b -> p (a b)"), op0=ALU.mult, op1=ALU.mult),
                waits=[t_r, x_ready])
        return t_h

    def load_wgu(self, wg, wu, f0):
        return self.load_pair(wg[:, f0:f0 + 256], wu[:, f0:f0 + 256])

    def load_wgu_old(self, wg, wu, f0):
        s = self.wgu_i % 2
        self.wgu_i += 1
        w = [self.wgu_free[s]]
        self.dma("pool", self.WG[s], wg[:, f0:f0 + 256].rearrange("(c p) f -> p c f", p=128),
                 self.s_wgu[s], waits=w)
        t = self.dma("pool", self.WU[s], wu[:, f0:f0 + 256].rearrange("(c p) f -> p c f", p=128),
                     self.s_wgu[s])
        return s, t

    def load_wd(self, wd, c0, n, dcol0):
        s = self.wd_i % 2
        self.wd_i += 1
        t = self.dma("pool", self.WD[s][:, 0:n, :],
                     wd[c0 * 128:(c0 + n) * 128, dcol0:dcol0 + 512].rearrange("(c p) d -> p c d", p=128),
                     self.s_wd[s], waits=[self.wd_free[s]])
        return s, t

    def ffn_tile(self, gain, wg, wu, wd, x_ready, pre_stats=None):
        X, H, AT = self.X, self.H, self.AT
        pend_wgu = [self.load_wgu(wg, wu, 0), self.load_wgu(wg, wu, 256)]
        next_f = 512
        pend_wd = [self.load_wd(wd, FGROUPS[0][0], FGROUPS[0][1], 0),
                   self.load_wd(wd, FGROUPS[0][0], FGROUPS[0][1], 512)]
        h_ready = self.norm_tile(gain, x_ready, pre_stats=pre_stats)
        last_x = None
        for gidx, (c0, n) in enumerate(FGROUPS):
            at_done = None
            for j in range(n):
                fc = c0 + j
                if fc % 2 == 0:
                    slot, wtok = pend_wgu.pop(0)
                for tb in range(2):
                    sl = slice(tb * 512, (tb + 1) * 512)
                    k, pp, pfree = self.get_pair()
                    t_pe = None
                    for wi, W in enumerate((self.WG, self.WU)):
                        for d in range(16):
                            t_pe = self.op("pe", lambda e, W=W, slot=slot, d=d, fc=fc, sl=sl, wi=wi, pp=pp: e.matmul(
                                pp[:, wi, :], W[slot][:, d, (fc % 2) * 128:(fc % 2) * 128 + 128], H[:, d, sl],
                                start=(d == 0), stop=(d == 15)),
                                waits=[wtok, h_ready, pfree] if (d == 0 and wi == 0) else (),
                                post=(d == 15 and wi == 1))
                    s = (2 * j + tb) % 2
                    t_act = self.op("act", lambda e, pp=pp, s=s: e.activation(
                        out=self.SG[s], in_=pp[:, 0, :], func=AF.Silu), waits=[t_pe, self.sg_free[s]])
                    t_dve = self.op("dve", lambda e, pp=pp, s=s, j=j, sl=sl: e.tensor_tensor(
                        out=AT[:, j, sl], in0=self.SG[s], in1=pp[:, 1, :], op=ALU.mult), waits=[t_act])
                    self.sg_free[s] = t_dve
                    self.pair_free[k] = t_dve
                    at_done = t_dve
                if fc % 2 == 1:
                    self.wgu_free[slot] = t_pe
                    if next_f < DFF:
                        pend_wgu.append(self.load_wgu(wg, wu, next_f))
                        next_f += 256
            for q4 in range(4):
                wslot, wdtok = pend_wd.pop(0)
                for dq in range(4):
                    dc = q4 * 4 + dq
                    k, pp, pfree = self.get_pair()
                    t_pe = None
                    for tb in range(2):
                        sl = slice(tb * 512, (tb + 1) * 512)
                        for j in range(n):
                            t_pe = self.op("pe", lambda e, pp=pp, tb=tb, wslot=wslot, j=j, dq=dq, sl=sl, n=n: e.matmul(
                                pp[:, tb, :], self.WD[wslot][:, j, dq * 128:(dq + 1) * 128], AT[:, j, sl],
                                start=(j == 0), stop=(j == n - 1)),
                                waits=[wdtok, at_done, pfree] if (j == 0 and tb == 0) else (),
                                post=(j == n - 1 and tb == 1))
                    t_dve = self.op("dve", lambda e, pp=pp, dc=dc: e.scalar_tensor_tensor(
                        out=X[:, dc, :], in0=pp.rearrange("p a b -> p (a b)"), scalar=0.5, in1=X[:, dc, :],
                        op0=ALU.mult, op1=ALU.add), waits=[t_pe])
                    self.pair_free[k] = t_dve
                    last_x = t_dve
                self.wd_free[wslot] = t_pe
                if q4 + 2 < 4:
                    pend_wd.append(self.load_wd(wd, c0, n, (q4 + 2) * 512))
                elif gidx + 1 < len(FGROUPS):
                    nc0, nn = FGROUPS[gidx + 1]
                    pend_wd.append(self.load_wd(wd, nc0, nn, (q4 - 2) * 512))
        assert not pend_wgu and not pend_wd
        return last_x

    def load_pair(self, apA, apB):
        s = self.wgu_i % 2
        self.wgu_i += 1
        self.dma("pool", self.WG[s], apA.rearrange("(c p) f -> p c f", p=128), self.s_wgu[s],
                 waits=[self.wgu_free[s]])
        t = self.dma("pool", self.WU[s], apB.rearrange("(c p) f -> p c f", p=128), self.s_wgu[s])
        return s, t

    def setup_stg(self):
        self.STG = [self.alloc(1024) for _ in range(2)]
        self.s_stg = [self.sem("stg%d" % i) for i in range(2)]
        self.stg_free = [None, None]
        self.stg_i = getattr(self, "stg_i", 0)
        self.vstg_free = [None, None]
        self.vstg_i = getattr(self, "vstg_i", 0)

    def proj(self, R, w, ngroups, r_ready, evac):
        pend = [self.load_pair(w[:, 0:256], w[:, 256:512])]
        t_pe = None
        for gi in range(ngroups):
            if gi + 1 < ngroups:
                c0 = (gi + 1) * 512
                pend.append(self.load_pair(w[:, c0:c0 + 256], w[:, c0 + 256:c0 + 512]))
            slot, wtok = pend.pop(0)
            for q in range(4):
                W = self.WG if q < 2 else self.WU
                k, pp, pfree = self.get_pair()
                for tb in range(2):
                    sl = slice(tb * 512, (tb + 1) * 512)
                    for d in range(16):
                        t_pe = self.op("pe", lambda e, W=W, slot=slot, d=d, q=q, sl=sl, tb=tb, pp=pp: e.matmul(
                            pp[:, tb, :], W[slot][:, d, (q % 2) * 128:(q % 2) * 128 + 128], R[:, d, sl],
                            start=(d == 0), stop=(d == 15)),
                            waits=[wtok, r_ready, pfree] if (d == 0 and tb == 0) else (),
                            post=(d == 15 and tb == 1))
                self.pair_free[k] = evac(gi * 4 + q, pp, t_pe)
            self.wgu_free[slot] = t_pe
        return t_pe

    def evac_to_dram(self, pp, t_pe, dst, bf16):
        s = self.stg_i % 2
        self.stg_i += 1
        eng = "act" if s == 0 else "dve"
        stg = self.STG[s]
        if bf16:
            stg = stg.bitcast(BF16)[:, 0:1024]
        src = pp.rearrange("p a b -> p (a b)")
        if eng == "act":
            t = self.op("act", lambda e, stg=stg, src=src: e.activation(out=stg, in_=src, func=AF.Copy),
                        waits=[t_pe, self.stg_free[s]])
        else:
            t = self.op("dve", lambda e, stg=stg, src=src: e.tensor_copy(out=stg, in_=src),
                        waits=[t_pe, self.stg_free[s]])
        self.stg_free[s] = self.dma("sp", dst, stg, self.s_stg[s], waits=[t])
        return t

    def final_norm_tile(self, gain, x_ready, dst, tok0):
        X, ACC, RSTD = self.X, self.ACC, self.RSTD
        t_r = self.norm_stats(x_ready)
        last = None
        for c in range(16):
            s = self.stg_i % 2
            self.stg_i += 1
            stg = self.STG[s]
            t = self.op("dve", lambda e, c=c, stg=stg: e.scalar_tensor_tensor(
                out=stg, in0=X[:, c, :], scalar=gain[:, c:c + 1],
                in1=RSTD.rearrange("p a b -> p (a b)"), op0=ALU.mult, op1=ALU.mult),
                waits=[t_r, self.stg_free[s]])
            self.stg_free[s] = self.dma("sp", dst[c * 128:(c + 1) * 128, tok0:tok0 + TT], stg,
                                        self.s_stg[s], waits=[t])
            last = t
        return last

    def norm_stats(self, x_ready):
        X, ACC, RSTD = self.X, self.ACC, self.RSTD
        last_acc = None
        for tb in range(2):
            sl = slice(tb * 512, (tb + 1) * 512)
            for c in range(16):
                if c == 0:
                    last = self.op("act", lambda e, c=c, sl=sl, tb=tb: e.activation(
                        out=ACC[:, tb, :], in_=X[:, c, sl], func=AF.Square),
                        waits=[x_ready, self.acc_free])
                else:
                    s = c % 2
                    t_sq = self.op("act", lambda e, c=c, sl=sl, s=s: e.activation(
                        out=self.SQ[s], in_=X[:, c, sl], func=AF.Square),
                        waits=[x_ready, self.sq_free[s]])
                    last = self.op("dve", lambda e, tb=tb, s=s: e.tensor_tensor(
                        out=ACC[:, tb, :], in0=self.SQ[s], in1=ACC[:, tb, :], op=ALU.add),
                        waits=[t_sq, last])
                    self.sq_free[s] = last
            last_acc = last
        k, pp, pfree = self.get_pair()
        t_pe = None
        for tb in range(2):
            t_pe = self.op("pe", lambda e, tb=tb, pp=pp: e.matmul(
                pp[:, tb, :], self.ones, ACC[:, tb, :], start=True, stop=True),
                waits=[last_acc, pfree, self.tok_ones])
        self.acc_free = t_pe
        t_act = self.op("act", lambda e, pp=pp: e.activation(
            out=RSTD, in_=pp, func=AF.Sqrt, bias=self.eps_ap, scale=1.0 / D),
            waits=[t_pe, self.rstd_free])
        self.pair_free[k] = t_act
        t_r = self.op("dve", lambda e: e.reciprocal(out=RSTD, in_=RSTD), waits=[t_act])
        return t_r

    def norm_tile(self, gain, x_ready, pre_stats=None):
        X, H, RSTD = self.X, self.H, self.RSTD
        t_r = pre_stats if pre_stats is not None else self.norm_stats(x_ready)
        t_h = None
        for c in range(16):
            t_h = self.op("dve", lambda e, c=c: e.scalar_tensor_tensor(
                out=H[:, c, :], in0=X[:, c, :], scalar=gain[:, c:c + 1],
                in1=RSTD.rearrange("p a b -> p (a b)"), op0=ALU.mult, op1=ALU.mult),
                waits=[t_r, x_ready, self.h_free])
        self.rstd_free = t_h
        return t_h

    def inproj_tile(self, w_in, h_ready, qT, kT, V, zr, tok0, need_q=True, need_zr=True, mid_cb=None):
        H = self.H

        def evac(chunk, pp, t_pe, base):
            zc = base + chunk
            if zc < 8:
                dst, bf = qT[zc * 128:(zc + 1) * 128, tok0:tok0 + TT], True
            elif zc < 16:
                dst, bf = kT[(zc - 8) * 128:(zc - 7) * 128, tok0:tok0 + TT], True
            else:
                dst, bf = zr[(zc - 24) * 128:(zc - 23) * 128, tok0:tok0 + TT], False
            return self.evac_to_dram(pp, t_pe, dst, bf)

        if need_q:
            t1 = self.proj(H, w_in[:, 0:2048], 4, h_ready, lambda c, pp, t: evac(c, pp, t, 0))
        else:
            t1 = self.proj(H, w_in[:, 1024:2048], 2, h_ready, lambda c, pp, t: evac(c, pp, t, 8))
        t_pe = None
        for vi in range(2):
            c0 = 2048 + vi * 512
            slot, wtok = self.load_pair(w_in[:, c0:c0 + 256], w_in[:, c0 + 256:c0 + 512])
            for tkb in range(TT // 128):
                k, pp, pfree = self.get_pair()
                for half, W in enumerate((self.WG, self.WU)):
                    for d in range(16):
                        t_pe = self.op("pe", lambda e, W=W, slot=slot, d=d, tkb=tkb, half=half, pp=pp: e.matmul(
                            pp[:, half, 0:256], H[:, d, tkb * 128:(tkb + 1) * 128], W[slot][:, d, :],
                            start=(d == 0), stop=(d == 15)),
                            waits=[wtok, h_ready, pfree] if (d == 0 and half == 0) else (),
                            post=(d == 15 and half == 1))
                s = self.vstg_i % 2
                self.vstg_i += 1
                stg = self.VSTG[s]
                t = None
                for half in range(2):
                    o_ = stg[:, 4 * half:4 * half + 4, 0:64]
                    i_ = pp[:, half, 0:256].rearrange("p (h c) -> p h c", c=64)
                    if half == 0:
                        t = self.op("act", lambda e, o_=o_, i_=i_: e.activation(out=o_, in_=i_, func=AF.Copy),
                                    waits=[t_pe, self.vstg_free[s], self.tok_vstg])
                    else:
                        t = self.op("dve", lambda e, o_=o_, i_=i_: e.tensor_copy(out=o_, in_=i_),
                                    waits=[t_pe, self.vstg_free[s], self.tok_vstg, t])
                dst = V[tok0 + tkb * 128:tok0 + (tkb + 1) * 128, vi * 520:(vi + 1) * 520]
                self.vstg_free[s] = self.dma("sp", dst, stg.rearrange("p h c -> p (h c)"), self.sem("vstg%d" % s),
                                             waits=[t])
                self.pair_free[k] = t
            self.wgu_free[slot] = t_pe
        if mid_cb is not None:
            mid_cb()
        if not need_zr:
            return t_pe
        t3 = self.proj(H, w_in[:, 3072:5120], 4, h_ready, lambda c, pp, t: evac(c, pp, t, 24))
        return t3

    def convpool_stage(self, zr, zrh, CW, PSC, PW, IC16, yT):
        E = 16
        prev = [None]

        def dv(fn, waits=()):
            t = self.op("dve", fn, waits=list(waits) + [prev[0]])
            prev[0] = t
            return t
        GB = self.alloc_f32(1, NT)[:, 0, :]
        GC = self.alloc(NT + E)
        CI = self.alloc(NT + E)
        T1 = self.alloc(NT)
        YC = [self.alloc(NT // 2).bitcast(BF16) for _ in range(2)]
        SB = self.alloc(NT + E)
        s_ld = self.sem("cp_ld")
        s_st = [self.sem("cp_st0"), self.sem("cp_st1")]
        yc_free = [None, None]
        yi = 0
        free = None
        for cc in range(4):
            self.dma("sp", GB, zr[cc * 128:(cc + 1) * 128, :], s_ld, waits=[free])
            self.dma("sp", GC[:, E:], zr[(4 + cc) * 128:(5 + cc) * 128, :], s_ld)
            self.dma("sp", GC[:, 0:E], zrh[cc * 128:(cc + 1) * 128, :], s_ld)
            self.dma("sp", CI[:, E:], zr[(8 + cc) * 128:(9 + cc) * 128, :], s_ld)
            ld = self.dma("sp", CI[:, 0:E], zrh[(4 + cc) * 128:(5 + cc) * 128, :], s_ld)
            dv(lambda e: e.tensor_tensor(out=GC, in0=GC, in1=CI, op=ALU.mult), waits=[ld])
            yield
            dv(lambda e, cc=cc: e.tensor_scalar(
                out=T1, in0=GC[:, E - 2:E - 2 + NT], scalar1=CW[:, cc, 0:1], scalar2=None, op0=ALU.mult))
            yield
            dv(lambda e, cc=cc: e.scalar_tensor_tensor(
                out=T1, in0=GC[:, E - 1:E - 1 + NT], scalar=CW[:, cc, 1:2], in1=T1, op0=ALU.mult, op1=ALU.add))
            yield
            dv(lambda e, cc=cc: e.scalar_tensor_tensor(
                out=T1, in0=GC[:, E:E + NT], scalar=CW[:, cc, 2:3], in1=T1, op0=ALU.mult, op1=ALU.add))
            yield
            s = yi % 2
            yi += 1
            t = dv(lambda e, s=s: e.tensor_tensor(out=YC[s], in0=T1, in1=GB, op=ALU.mult),
                   waits=[yc_free[s]])
            free = t
            yc_free[s] = self.dma("sp", yT[(8 + cc) * 128:(9 + cc) * 128, :], YC[s], s_st[s], waits=[t])
            yield
        PIN = GC
        SA = CI
        PB = T1.bitcast(BF16)[:, 0:NT]
        for g in range(4):
            w = 2 << g
            self.dma("sp", PIN[:, E:], zr[(12 + g) * 128:(13 + g) * 128, :], s_ld, waits=[free])
            ld = self.dma("sp", PIN[:, 0:E], zrh[(8 + g) * 128:(9 + g) * 128, :], s_ld)
            cur, bufs, look = PIN, [SA, SB], 0
            first = True
            for k in range(g + 1):
                step = 1 << k
                nl = look + step
                nxt = bufs[k % 2]
                dv(lambda e, cur=cur, nxt=nxt, nl=nl, look=look, step=step: e.tensor_tensor(
                    out=nxt[:, nl:], in0=cur[:, nl:], in1=cur[:, look:NT + E - step], op=ALU.add),
                    waits=[ld] if first else ())
                yield
                first = False
                cur, look = nxt, nl
            tmp = bufs[(g + 1) % 2]
            dv(lambda e, cur=cur, tmp=tmp, w=w: e.scalar_tensor_tensor(
                out=tmp[:, E:], in0=cur[:, E:], scalar=1.0 / w, in1=PIN[:, E:], op0=ALU.mult, op1=ALU.subtract))
            yield
            dv(lambda e, cur=cur, g=g: e.tensor_tensor(
                out=cur[:, E:2 * E], in0=cur[:, E:2 * E], in1=IC16[:, g, :], op=ALU.mult))
            dv(lambda e, cur=cur, tmp=tmp: e.tensor_tensor(
                out=tmp[:, E:2 * E], in0=cur[:, E:2 * E], in1=PIN[:, E:2 * E], op=ALU.subtract))
            t_pb = dv(lambda e, tmp=tmp: e.tensor_copy(out=PB, in_=tmp[:, E:]))
            free = t_pb
            yield
            s = yi % 2
            yi += 1
            t = None
            for qtr in range(4):
                k, pb, pfree = self.get_bank()
                c0_ = qtr * 512
                t_pe = self.op("pe", lambda e, g=g, pb=pb, c0_=c0_: e.matmul(
                    pb, PW[:, g, :], PB[:, c0_:c0_ + 512], start=True, stop=True),
                    waits=[t_pb, pfree, self.tok_pw])
                t = dv(lambda e, g=g, pb=pb, s=s, c0_=c0_: e.tensor_scalar(
                    out=YC[s][:, c0_:c0_ + 512], in0=pb,
                    scalar1=PSC[:, g:g + 1], scalar2=None, op0=ALU.mult), waits=[t_pe, yc_free[s]])
                self.bank_free[k] = t
                free = t_pe
                yield
            yc_free[s] = self.dma("sp", yT[(12 + g) * 128:(13 + g) * 128, :], YC[s], s_st[s], waits=[t])

    def get_bank(self):
        b_ = self.nbank % 8
        self.nbank += 1
        return b_, self.ps[:, b_, :], self.bank_free[b_]

    def attention_stage(self, qT, kT, kTh, V, Vh, MASK, yT, bg=None):
        QT = [self.alloc_bf16(1, NT)[:, 0, :] for _ in range(2)]
        KT = [self.alloc_bf16(1, 2 * NT)[:, 0, :] for _ in range(2)]
        V1 = [self.alloc_bf16(32, 130) for _ in range(2)]
        V4 = [self.alloc_bf16(32, 130) for _ in range(2)]
        V16 = [self.alloc_bf16(32, 130) for _ in range(2)]
        ACC = [self.alloc(NT) for _ in range(2)]
        SEL = self.alloc(64)
        self.op("dve", lambda e: e.memset(SEL[0:64, :], 0.0))
        t_sel = self.op("dve", lambda e: e.memset(SEL[64:65, :], 1.0))
        YH = [self.alloc_bf16(1, NT)[:, 0, :] for _ in range(2)]
        ET = [self.alloc_bf16(1, 256)[:, 0, :] for _ in range(8)]
        s_ld = [self.sem("at_ld0"), self.sem("at_ld1")]
        s_yh = [self.sem("at_yh0"), self.sem("at_yh1")]
        att_free = [None, None]
        acc_free = [None, None]
        yh_free = [None, None]
        et_free = [None] * 8
        bank_free = self.bank_free
        get_bank = self.get_bank
        pending = []
        fin_pe = [None]
        LT = [self.alloc(16) for _ in range(2)]
        s_lt = [self.sem("at_lt0"), self.sem("at_lt1")]
        nunit = [0]

        def bg_step():
            if bg is not None:
                next(bg, None)

        def issue_loads(hc):
            s = hc % 2
            rs_ = slice(hc * 128, (hc + 1) * 128)
            cs = slice(hc * 130, (hc + 1) * 130)
            w = [att_free[s]]
            self.dma("sp", QT[s], qT[rs_, :], s_ld[s], waits=w)
            self.dma("sp", KT[s][:, 0:NT], kTh[rs_, :], s_ld[s])
            self.dma("sp", KT[s][:, NT:], kT[rs_, :], s_ld[s])
            for kb, src in enumerate((Vh, V)):
                self.dma("sp", V1[s][:, 16 * kb:16 * kb + 16, :],
                         src[:, cs].rearrange("(b p) c -> p b c", p=128), s_ld[s])
                self.dma("sp", V16[s][:, 16 * kb:16 * kb + 16, :],
                         src[:, cs].rearrange("(i r) c -> i r c", r=16), s_ld[s])
                v4 = src[:, cs].rearrange("(kb i r) c -> r i kb c", i=128, r=4)
                for r in range(4):
                    t = self.dma("sp", V4[s][:, r * 8 + 4 * kb:r * 8 + 4 * kb + 4, :], v4[r], s_ld[s])
            return t

        ld_tok = {0: issue_loads(0)}
        hidx = 0
        ei = 0
        for hc in range(8):
            s = hc % 2
            if hc + 1 < 8:
                ld_tok[hc + 1] = issue_loads(hc + 1)
            ld = ld_tok[hc]
            last_pe = None
            for hd in range(2):
                a = hidx % 2
                hidx += 1
                ps_ = slice(hd * 64, (hd + 1) * 64)
                units = []
                for (st, Vb) in ((1, V1[s]), (4, V4[s]), (16, V16[s])):
                    nblk = 16 // st
                    for r in range(st):
                        for b in range(nblk):
                            Bq = nblk + b
                            qs = slice(r + st * 128 * b, r + st * 128 * b + st * 127 + 1, st)
                            kts, vts = [], []
                            for kb in (Bq - 1, Bq):
                                kts.append(slice(r + st * 128 * kb, r + st * 128 * kb + st * 127 + 1, st))
                                if st == 1:
                                    vts.append(kb)
                                elif st == 4:
                                    vts.append(r * 8 + kb)
                                else:
                                    vts.append(kb * 16 + r)
                            units.append((st, qs, kts, vts, Vb, b == 0))
                state = {}

                def emit_S(n):
                    nonlocal ei
                    st, qs, kts, vts, Vb, first = units[n]
                    k, pp, pfree = get_bank()
                    t_s = None
                    m = 1 if first else 0
                    for kbi in range(2):
                        t_s = self.op("pe", lambda e, pp=pp, kbi=kbi, kts=kts, qs=qs, s=s, ps_=ps_: e.matmul(
                            pp[:, kbi * 128:(kbi + 1) * 128], KT[s][ps_, kts[kbi]], QT[s][ps_, qs],
                            start=True, stop=True), waits=[ld, pfree] if kbi == 0 else (), post=(kbi == 1))
                    ex = ei % 8
                    ei += 1
                    t_e = self.op("act", lambda e, pp=pp, ex=ex: e.activation(
                        out=ET[ex], in_=pp[:, 0:256], func=AF.Exp, scale=0.125),
                        waits=[t_s, et_free[ex]])
                    meng = "dve" if n % 3 == 2 else "pool"
                    t_m = self.op(meng, lambda e, ex=ex, m=m: e.tensor_tensor(
                        out=ET[ex], in0=ET[ex], in1=MASK[:, m, :], op=ALU.mult), waits=[t_e, self.tok_mask])
                    state[n] = (k, pp, ex, t_m)

                def emit_NL(n):
                    st, qs, kts, vts, Vb, first = units[n]
                    k, pp, ex, t_m = state.pop(n)
                    t_nl = None
                    for kbi in range(2):
                        t_nl = self.op("pe", lambda e, pp=pp, kbi=kbi, vts=vts, Vb=Vb, ex=ex, hd=hd: e.matmul(
                            pp[0:65, 256:384], Vb[:, vts[kbi], hd * 65:hd * 65 + 65],
                            ET[ex][:, kbi * 128:(kbi + 1) * 128],
                            start=(kbi == 0), stop=(kbi == 1)), waits=[t_m] if kbi == 0 else (), post=(kbi == 1))
                    et_free[ex] = t_nl
                    src = pp[0:65, 256:384]
                    if st == 1:
                        t_acc = self.op("dve", lambda e, src=src, qs=qs, a=a: e.tensor_copy(
                            out=ACC[a][0:65, qs], in_=src), waits=[t_nl, acc_free[a]])
                    else:
                        t_acc = self.op("dve", lambda e, src=src, qs=qs, a=a: e.tensor_tensor(
                            out=ACC[a][0:65, qs], in0=src, in1=ACC[a][0:65, qs], op=ALU.add),
                            waits=[t_nl])
                    bank_free[k] = t_acc
                    return t_nl, t_acc

                LA = 5
                t_acc = None
                for n in range(len(units) + LA):
                    if n < len(units):
                        emit_S(n)
                    if n >= LA:
                        last_pe, t_acc = emit_NL(n - LA)
                    nunit[0] += 1
                    if nunit[0] % 8 == 0:
                        bg_step()
                    if nunit[0] % 8 == 4:
                        for g_ in pending:
                            next(g_, None)
                def finalize(a=a, hc=hc, hd=hd, t_acc=t_acc):
                    d1 = self.dma("sp", self.lscr[a:a + 1, :], ACC[a][64:65, :], s_lt[a], waits=[t_acc])
                    d2 = self.dma("sp", LT[a], self.lscr[a].rearrange("(p j) -> p j", j=16), s_lt[a], waits=[d1])
                    yield
                    t_rc = self.op("dve", lambda e: e.reciprocal(out=LT[a], in_=LT[a]), waits=[d2])
                    d3 = self.dma("sp", self.lscr2[a].rearrange("(p j) -> p j", j=16), LT[a], s_lt[a], waits=[t_rc])
                    d4 = self.dma("sp", ACC[a][64:65, :], self.lscr2[a:a + 1, :], s_lt[a], waits=[d3])
                    yield
                    t_y = None
                    for qtr in range(4):
                        k, pp, pfree = get_bank()
                        c0_ = qtr * 512
                        t_b = self.op("pe", lambda e, pp=pp, c0_=c0_: e.matmul(
                            pp[0:64, :], SEL[0:65, :], ACC[a][0:65, c0_:c0_ + 512], start=True, stop=True),
                            waits=[d4, pfree, t_sel])
                        t_y = self.op("dve", lambda e, pp=pp, c0_=c0_: e.tensor_tensor(
                            out=YH[a][0:64, c0_:c0_ + 512], in0=ACC[a][0:64, c0_:c0_ + 512],
                            in1=pp[0:64, :], op=ALU.mult),
                            waits=[t_b, yh_free[a]])
                        bank_free[k] = t_y
                        yield
                    acc_free[a] = t_y
                    yh_free[a] = self.dma("sp", yT[hc * 128 + hd * 64:hc * 128 + hd * 64 + 64, :], YH[a][0:64, :],
                                          s_yh[a], waits=[t_y])
                    fin_pe[0] = t_b
                for g_ in pending:
                    for _ in g_:
                        pass
                pending.clear()
                pending.append(finalize())
            att_free[s] = last_pe
        for g_ in pending:
            for _ in g_:
                pass
        if bg is not None:
            for _ in bg:
                pass

    def load_y_tile(self, yT, tok0):
        t = None
        for i in range(4):
            t = self.dma("sp", self.H[:, 4 * i:4 * i + 4, :],
                         yT[i * 512:(i + 1) * 512, tok0:tok0 + TT].rearrange("(c p) t -> p c t", p=128),
                         self.sem("yld"), waits=[self.h_free] if i == 0 else ())
        return t

    def outproj_tile(self, w_out, x_ready, y_ready):
        X = self.X
        last = [None]

        def evac(chunk, pp, t_pe):
            t = self.op("dve", lambda e, pp=pp, chunk=chunk: e.tensor_tensor(
                out=X[:, chunk, :], in0=pp.rearrange("p a b -> p (a b)"), in1=X[:, chunk, :], op=ALU.add),
                waits=[t_pe, x_ready])
            last[0] = t
            return t
        t_pe = self.proj(self.H, w_out, 4, y_ready, evac)
        self.h_free = t_pe
        return last[0]


def _stage_A(B, regions, xsrc, x1, qT, kT, V, zr, G1, GM, wg, wu, wd, w_in, need_q):
    tiles = [(r, t) for r in regions for t in range(NT // TT)]
    xr = B.load_x(xsrc[tiles[0][0]], tiles[0][1] * TT)
    pre = None
    for i, (r, t) in enumerate(tiles):
        done = B.ffn_tile(G1, wg, wu, wd, xr, pre_stats=pre)
        st = B.store_x(x1[r], t * TT, [done])
        h_ready = B.norm_tile(GM, done)
        B.x_free = [st, h_ready]
        nxt = {}
        if i + 1 < len(tiles):
            nxt["xr"] = B.load_x(xsrc[tiles[i + 1][0]], tiles[i + 1][1] * TT)

            def mid(nxt=nxt):
                nxt["pre"] = B.norm_stats(nxt["xr"])
        else:
            mid = None
        B.h_free = B.inproj_tile(w_in, h_ready, qT[r], kT[r], V[r], zr[r], t * TT, need_q=need_q[r],
                                 need_zr=(need_q[r] or t == NT // TT - 1), mid_cb=mid)
        xr, pre = nxt.get("xr"), nxt.get("pre")
        B.x_free = []


def _stage_B(B, base, r, x1, qT, kT, V, zr, yT, x3, outT, CW, PSC, PW, IC16, MASK, G2, GF,
             w_out, wg, wu, wd, final):
    B.full_barrier()
    B.aoff = base
    zrh = zr[r + 1][512:2048, NT - 16:NT]
    B.bank_free = [None] * 8
    B.nbank = 0
    bg = B.convpool_stage(zr[r], zrh, CW, PSC, PW, IC16, yT)
    B.attention_stage(qT[r], kT[r], kT[r + 1], V[r], V[r + 1], MASK, yT, bg=bg)
    B.full_barrier()
    B.aoff = base
    B.setup_ffn_bufs()
    B.setup_stg()
    for t in range(NT // TT):
        xr = B.load_x(x1[r], t * TT)
        yr = B.load_y_tile(yT, t * TT)
        x2 = B.outproj_tile(w_out, xr, yr)
        done = B.ffn_tile(G2, wg, wu, wd, x2)
        B.h_free = done
        if final:
            fin = B.final_norm_tile(GF, done, outT, t * TT)
            B.x_free = [fin]
        else:
            st = B.store_x(x3[r], t * TT, [done])
            B.x_free = [st]


NCONST = 64 + 24 + 8 + 128 + 16


def build_fused():
    nc = bass.Bass("TRN2", target_bir_lowering=False)
    xin = nc.dram_tensor("xin", [3, D, NT], F32, kind="ExternalInput").ap()
    consts = nc.dram_tensor("consts", [128, NCONST + 32], F32, kind="ExternalInput").ap()
    masks = nc.dram_tensor("masks", [128, 4, 256], BF16, kind="ExternalInput").ap()
    W = []
    for l in range(2):
        d = {}
        for nm, shp in (("wg1", [D, DFF]), ("wu1", [D, DFF]), ("wd1", [DFF, D]), ("w_in", [D, DIN]),
                        ("w_out", [D, D]), ("wg2", [D, DFF]), ("wu2", [D, DFF]), ("wd2", [DFF, D]),
                        ("pool_w", [4, 128, 128])):
            d[nm] = nc.dram_tensor("%s_%d" % (nm, l), shp, F32, kind="ExternalInput").ap()
        W.append(d)
    outT = nc.dram_tensor("outT", [D, NT], F32, kind="ExternalOutput").ap()
    x1 = [nc.dram_tensor("x1_%d" % r, [D, NT], F32).ap() for r in range(3)]
    x3 = [nc.dram_tensor("x3_%d" % r, [D, NT], F32).ap() for r in range(2)]
    qT = [nc.dram_tensor("q_%d" % r, [1024, NT], BF16).ap() for r in range(3)]
    kT = [nc.dram_tensor("k_%d" % r, [1024, NT], BF16).ap() for r in range(3)]
    V = [nc.dram_tensor("v_%d" % r, [NT, 1040], BF16).ap() for r in range(3)]
    zr = [nc.dram_tensor("zr_%d" % r, [2048, NT], F32).ap() for r in range(3)]
    yT = nc.dram_tensor("yT", [D, NT], BF16).ap()
    lscr = nc.dram_tensor("lscr", [2, NT], F32).ap()
    lscr2 = nc.dram_tensor("lscr2", [2, NT], F32).ap()
    xsrc = [xin[r] for r in range(3)]
    with ExitStack() as stack:
        B = Builder(nc, stack)
        B.lscr, B.lscr2 = lscr, lscr2
        B.x_free = []
        B.acc_free = None
        B.rstd_free = None
        B.h_free = None
        B.eps_ap = B.alloc(1)
        t_eps = B.op("dve", lambda e: e.memset(B.eps_ap, EPS))
        B.wait_only("act", [t_eps])
        C = B.alloc(NCONST + 32)
        MASK = B.alloc_bf16(4, 256)
        PW = [B.alloc_bf16(4, 128) for _ in range(2)]
        s_c = B.sem("consts")
        B.dma("sp", C, consts[:, :], s_c)
        B.tok_mask = B.dma("sp", MASK, masks[:, :, :], s_c)
        B.wait_only("dve", [B.tok_mask])
        s_pw = B.sem("pw")
        for l in range(2):
            B.tok_pw = B.dma("pool", PW[l], W[l]["pool_w"].rearrange("g c d -> c g d"), s_pw)
        G = lambda i: C[:, 16 * i:16 * i + 16]
        CWs = [C[:, 112 + 12 * l:112 + 12 * l + 12].rearrange("p (a b) -> p a b", a=4) for l in range(2)]
        PSCs = [C[:, 136 + 4 * l:136 + 4 * l + 4] for l in range(2)]
        IC = [C[:, 144 + 64 * k:144 + 64 * k + 64].rearrange("p (a b) -> p a b", a=4) for k in range(2)]
        MK = [MASK[:, 2 * k:2 * k + 2, :] for k in range(2)]
        base = B.aoff
        B.setup_ffn_bufs()
        B.setup_stg()
        _stage_A(B, [2, 1, 0], xsrc, x1, qT, kT, V, zr, G(0), G(1), W[0]["wg1"], W[0]["wu1"], W[0]["wd1"],
                 W[0]["w_in"], need_q={2: False, 1: True, 0: True})
        for r in (1, 0):
            _stage_B(B, base, r, x1, qT, kT, V, zr, yT, x3, None, CWs[0], PSCs[0], PW[0], IC[r], MK[r],
                     G(2), None, W[0]["w_out"], W[0]["wg2"], W[0]["wu2"], W[0]["wd2"], final=False)
        B.full_barrier()
        _stage_A(B, [1, 0], x3, x1, qT, kT, V, zr, G(3), G(4), W[1]["wg1"], W[1]["wu1"], W[1]["wd1"],
                 W[1]["w_in"], need_q={1: False, 0: True})
        _stage_B(B, base, 0, x1, qT, kT, V, zr, yT, x3, outT, CWs[1], PSCs[1], PW[1], IC[0], MK[0],
                 G(5), G(6), W[1]["w_out"], W[1]["wg2"], W[1]["wu2"], W[1]["wd2"], final=True)
        B.full_barrier()
        B.emit()
    return nc


_CACHE = {}


def _chunked(v):
    return np.ascontiguousarray(np.asarray(v, np.float32).reshape(-1, 128).T)


def kernel(x, ffn1_norm, ffn1_w_gate, ffn1_w_up, ffn1_w_down, mix_norm, w_in, conv_w,
           pool_w, pool_scale, w_out, ffn2_norm, ffn2_w_gate, ffn2_w_up, ffn2_w_down, final_norm):
    f = lambda a: np.ascontiguousarray(np.asarray(a, np.float32))
    x = f(x)
    cores = list(range(NCORES))
    if "nc" not in _CACHE:
        _CACHE["nc"] = build_fused()
    nc = _CACHE["nc"]
    jj = np.arange(128)[:, None]
    ii = np.arange(128)[None, :]
    m_prev = (jj >= ii).astype(np.float32)
    m_cur = (jj <= ii).astype(np.float32)
    wts = {}
    for l in range(2):
        wts.update({"wg1_%d" % l: f(ffn1_w_gate[l]), "wu1_%d" % l: f(ffn1_w_up[l]), "wd1_%d" % l: f(ffn1_w_down[l]),
                    "w_in_%d" % l: f(w_in[l]), "w_out_%d" % l: f(w_out[l]), "wg2_%d" % l: f(ffn2_w_gate[l]),
                    "wu2_%d" % l: f(ffn2_w_up[l]), "wd2_%d" % l: f(ffn2_w_down[l]), "pool_w_%d" % l: f(pool_w[l])})
    gains = [ffn1_norm[0], mix_norm[0], ffn2_norm[0], ffn1_norm[1], mix_norm[1], ffn2_norm[1], final_norm]
    cws = []
    for l in range(2):
        cw = np.asarray(conv_w[l], np.float32)
        cws.append(np.ascontiguousarray(cw.T.reshape(4, 128, 3).transpose(1, 0, 2)).reshape(128, 12))
    ins = []
    for c in cores:
        b, p = c // 4, c % 4
        xin = np.zeros((3, D, NT), np.float32)
        for r in range(3):
            if p - r >= 0:
                xin[r] = x[b, (p - r) * NT:(p - r + 1) * NT, :].T
        mk = np.zeros((128, 4, 256), np.float32)
        ics = []
        for r in range(2):
            halo_virtual = (p - r - 1) < 0
            mk[:, 2 * r, 0:128] = m_prev
            mk[:, 2 * r, 128:256] = m_cur
            mk[:, 2 * r + 1, 0:128] = 0.0 if halo_virtual else m_prev
            mk[:, 2 * r + 1, 128:256] = m_cur
            pos = max(p - r, 0) * NT + np.arange(16)
            ics.append(np.stack([1.0 / np.minimum(pos + 1, w) for w in (2, 4, 8, 16)]).astype(np.float32).reshape(1, 64))
        cst = np.concatenate(
            [_chunked(g) for g in gains] + cws + [_chunked(pool_scale[0]), _chunked(pool_scale[1])]
            + [np.broadcast_to(ic, (128, 64)) for ic in ics], axis=1)
        assert cst.shape == (128, NCONST + 32), cst.shape
        ins.append(dict(xin=xin, consts=np.ascontiguousarray(cst.astype(np.float32)),
                        masks=mk.astype(ml_dtypes.bfloat16), **wts))
    res = run_bass_kernel_spmd(nc, ins, core_ids=cores).results
    y = np.empty((2, 4 * NT, D), np.float32)
    for c in cores:
        y[c // 4, (c % 4) * NT:(c % 4 + 1) * NT, :] = res[c]["outT"].T
    return y
```

```python
import numpy as np
import ml_dtypes
from contextlib import ExitStack
import concourse.bass as bass
import concourse.mybir as mybir
from concourse.bass_utils import run_bass_kernel_spmd

F32 = mybir.dt.float32
BF16 = mybir.dt.bfloat16
AF = mybir.ActivationFunctionType
ALU = mybir.AluOpType

D = 2048
NT = 2048
TT = 1024
DFF = 5632
DIN = 5120
NCORES = 8
EPS = 1e-6
ENGS = ["pe", "act", "dve", "pool", "sp"]
FGROUPS = [(0, 12), (12, 12), (24, 12), (36, 8)]


class Sem:
    def __init__(self, h, name):
        self.h, self.n, self.name = h, 0, name


class Builder:
    def __init__(self, nc, stack):
        self.nc, self.stack = nc, stack
        self.streams = {e: [] for e in ENGS}
        self.cnt = {e: self.sem("c_" + e) for e in ["pe", "act", "dve", "pool"]}
        self.arena_words = 51 * 1024 + 512
        self.arena = stack.enter_context(nc.sbuf_tensor("arena", [128, self.arena_words], F32))
        self.aoff = 0
        self.ps = stack.enter_context(nc.psum_tensor("ps", [128, 8, 512], F32))
        self.pair_free = [None] * 4
        self.pair_i = 0
        self.nsem = 0
        self._st_i = 0
        self.fused_stats = None

    def sem(self, name):
        if not hasattr(self, "_sems"):
            self._sems = {}
        if name not in self._sems:
            self._sems[name] = Sem(self.stack.enter_context(self.nc.semaphore(name)), name)
        return self._sems[name]

    def full_barrier(self):
        toks = [(s_, s_.n) for s_ in self._sems.values() if s_.n > 0]
        for e in ENGS:
            self.wait_only(e, toks)
        self.pair_free = [None] * 4
        for nm in ("wgu_free", "wd_free", "sg_free", "stg_free", "vstg_free"):
            if hasattr(self, nm):
                setattr(self, nm, [None, None])
        if hasattr(self, "sg_free"):
            self.sq_free = self.sg_free
        self.x_free = []
        self.acc_free = None
        self.rstd_free = None
        self.h_free = None

    def alloc(self, words):
        off = self.aoff
        self.aoff += words
        assert self.aoff <= self.arena_words, (self.aoff, self.arena_words)
        return self.arena[:, off:off + words]

    def alloc_f32(self, *shape):
        n = int(np.prod(shape))
        ap = self.alloc(n)
        if len(shape) == 2:
            return ap.rearrange("p (a b) -> p a b", a=shape[0])
        if len(shape) == 3:
            return ap.rearrange("p (a b c) -> p a b c", a=shape[0], b=shape[1])
        return ap

    def alloc_bf16(self, *shape):
        n = int(np.prod(shape))
        assert n % 2 == 0
        ap = self.alloc(n // 2).bitcast(BF16)
        if len(shape) == 2:
            return ap.rearrange("p (a b) -> p a b", a=shape[0])
        if len(shape) == 3:
            return ap.rearrange("p (a b c) -> p a b c", a=shape[0], b=shape[1])
        return ap

    def op(self, eng, fn, waits=(), post=True):
        sem = None
        tok = None
        if post:
            sem = self.cnt[eng]
            sem.n += 1
            tok = (sem, sem.n)
        self.streams[eng].append((fn, [w for w in waits if w is not None], sem, 1))
        return tok

    def dma(self, eng, out, in_, sem, waits=()):
        sem.n += 16
        self.streams[eng].append(
            (lambda e, out=out, in_=in_: e.dma_start(out=out, in_=in_),
             [w for w in waits if w is not None], sem, 16))
        return (sem, sem.n)

    def wait_only(self, eng, waits):
        self.streams[eng].append((None, [w for w in waits if w is not None], None, 0))

    def get_pair(self):
        k = self.pair_i % 4
        self.pair_i += 1
        return k, self.ps[:, 2 * k:2 * k + 2, :], self.pair_free[k]

    def barrier(self, extra=()):
        toks = [(s, s.n) for s in self.cnt.values() if s.n > 0] + list(extra)
        for e in ENGS:
            self.wait_only(e, toks)

    def emit(self):
        nc = self.nc
        with nc.Block() as block:
            def mk(name):
                items = self.streams[name]

                def run(e):
                    seen = {}
                    for fn, waits, sem, inc in items:
                        for (s, v) in waits:
                            if seen.get(s.name, 0) < v:
                                e.wait_ge(s.h, v)
                                seen[s.name] = v
                        if fn is not None:
                            ins = fn(e)
                            if sem is not None:
                                ins.then_inc(sem.h, inc)
                return run
            block.tensor(mk("pe"))
            block.scalar(mk("act"))
            block.vector(mk("dve"))
            block.gpsimd(mk("pool"))
            block.sync(mk("sp"))

    def setup_ffn_bufs(self):
        self.X = self.alloc_f32(16, TT)
        self.H = self.alloc_bf16(16, TT)
        self.AT = self.alloc_bf16(12, TT)
        self.WG = [self.alloc_bf16(16, 256) for _ in range(2)]
        self.WU = [self.alloc_bf16(16, 256) for _ in range(2)]
        self.WD = [self.alloc_bf16(12, 512) for _ in range(2)]
        self.ACC = self.alloc_f32(2, 512)
        self.RSTD = self.alloc_f32(2, 512)
        self.SG = [self.alloc(512) for _ in range(2)]
        self.SQ = self.SG
        self.VSTG = [self.alloc(260).bitcast(BF16).rearrange("p (h c) -> p h c", c=65) for _ in range(2)]
        self.tok_vstg = None
        for i_ in range(2):
            self.tok_vstg = self.op("dve", lambda e, i_=i_: e.memset(self.VSTG[i_], 1.0))
        self.ones = self.alloc(128)
        self.s_wgu = [self.sem("wgu%d" % i) for i in range(2)]
        self.s_wd = [self.sem("wd%d" % i) for i in range(2)]
        self.s_xld = self.sem("xld")
        self.s_xst = self.sem("xst")
        self.wgu_free = [None, None]
        self.wd_free = [None, None]
        self.wgu_i = getattr(self, "wgu_i", 0)
        self.wd_i = getattr(self, "wd_i", 0)
        self.sg_free = [None, None]
        self.sq_free = self.sg_free
        self.x_free = []
        self.tok_ones = self.op("dve", lambda e: e.memset(self.ones, 1.0))

    def load_x(self, src, tok0):
        toks = []
        for i in range(4):
            t = self.dma("sp", self.X[:, 4 * i:4 * i + 4, :],
                         src[i * 512:(i + 1) * 512, tok0:tok0 + TT].rearrange("(c p) t -> p c t", p=128),
                         self.s_xld, waits=list(self.x_free) if i == 0 else ())
            toks.append(t)
        return toks[-1]

    def store_x(self, dst, tok0, waits):
        t = None
        for i in range(4):
            t = self.dma("sp", dst[i * 512:(i + 1) * 512, tok0:tok0 + TT].rearrange("(c p) t -> p c t", p=128),
                         self.X[:, 4 * i:4 * i + 4, :], self.s_xst, waits=waits if i == 0 else ())
        return t

    def norm_tile_old(self, gain, x_ready):
        X, H, ACC, RSTD = self.X, self.H, self.ACC, self.RSTD
        last_acc = None
        for tb in range(2):
            sl = slice(tb * 512, (tb + 1) * 512)
            for c in range(16):
                if c == 0:
                    t_sq = self.op("act", lambda e, c=c, sl=sl, tb=tb: e.activation(
                        out=ACC[:, tb, :], in_=X[:, c, sl], func=AF.Square),
                        waits=[x_ready, last_acc if tb == 0 else None])
                    last = t_sq
                else:
                    s = c % 2
                    t_sq = self.op("act", lambda e, c=c, sl=sl, s=s: e.activation(
                        out=self.SQ[s], in_=X[:, c, sl], func=AF.Square),
                        waits=[x_ready, self.sq_free[s]])
                    last = self.op("dve", lambda e, tb=tb, s=s: e.tensor_tensor(
                        out=ACC[:, tb, :], in0=self.SQ[s], in1=ACC[:, tb, :], op=ALU.add),
                        waits=[t_sq, last])
                    self.sq_free[s] = last
            last_acc = last
        k, pp, pfree = self.get_pair()
        t_pe = None
        for tb in range(2):
            t_pe = self.op("pe", lambda e, tb=tb, pp=pp: e.matmul(
                pp[:, tb, :], self.ones.rearrange("p (a b) -> p a b", a=1)[:, 0, :], ACC[:, tb, :],
                start=True, stop=True), waits=[last_acc, pfree, self.tok_ones])
        t_act = self.op("act", lambda e, pp=pp: e.activation(
            out=RSTD, in_=pp, func=AF.Sqrt, bias=self.eps_ap, scale=1.0 / D), waits=[t_pe])
        self.pair_free[k] = t_act
        t_r = self.op("dve", lambda e: e.reciprocal(out=RSTD, in_=RSTD), waits=[t_act])
        t_h = None
        for c in range(16):
            t_h = self.op("dve", lambda e, c=c: e.scalar_tensor_tensor(
                out=H[:, c, :], in0=X[:, c, :], scalar=gain[:, c:c + 1],
                in1=RSTD.rearrange("p a b -> p (a b)"), op0=ALU.mult, op1=ALU.mult),
                waits=[t_r, x_ready])
        return t_h

    def load_wgu(self, wg, wu, f0):
        return self.load_pair(wg[:, f0:f0 + 256], wu[:, f0:f0 + 256])

    def load_wgu_old(self, wg, wu, f0):
        s = self.wgu_i % 2
        self.wgu_i += 1
        w = [self.wgu_free[s]]
        self.dma("pool", self.WG[s], wg[:, f0:f0 + 256].rearrange("(c p) f -> p c f", p=128),
                 self.s_wgu[s], waits=w)
        t = self.dma("pool", self.WU[s], wu[:, f0:f0 + 256].rearrange("(c p) f -> p c f", p=128),
                     self.s_wgu[s])
        return s, t

    def load_wd(self, wd, c0, n, dcol0):
        s = self.wd_i % 2
        self.wd_i += 1
        t = self.dma("pool", self.WD[s][:, 0:n, :],
                     wd[c0 * 128:(c0 + n) * 128, dcol0:dcol0 + 512].rearrange("(c p) d -> p c d", p=128),
                     self.s_wd[s], waits=[self.wd_free[s]])
        return s, t

    def ffn_tile(self, gain, wg, wu, wd, x_ready, pre_stats=None, fuse_stats=False):
        X, H, AT = self.X, self.H, self.AT
        pend_wgu = [self.load_wgu(wg, wu, 0), self.load_wgu(wg, wu, 256)]
        next_f = 512
        pend_wd = [self.load_wd(wd, FGROUPS[0][0], FGROUPS[0][1], 0),
                   self.load_wd(wd, FGROUPS[0][0], FGROUPS[0][1], 512)]
        h_ready = self.norm_tile(gain, x_ready, pre_stats=pre_stats)
        last_x = None
        for gidx, (c0, n) in enumerate(FGROUPS):
            at_done = None
            for j in range(n):
                fc = c0 + j
                if fc % 2 == 0:
                    slot, wtok = pend_wgu.pop(0)
                for tb in range(2):
                    sl = slice(tb * 512, (tb + 1) * 512)
                    k, pp, pfree = self.get_pair()
                    t_pe = None
                    for wi, W in enumerate((self.WG, self.WU)):
                        for d in range(16):
                            t_pe = self.op("pe", lambda e, W=W, slot=slot, d=d, fc=fc, sl=sl, wi=wi, pp=pp: e.matmul(
                                pp[:, wi, :], W[slot][:, d, (fc % 2) * 128:(fc % 2) * 128 + 128], H[:, d, sl],
                                start=(d == 0), stop=(d == 15)),
                                waits=[wtok, h_ready, pfree] if (d == 0 and wi == 0) else (),
                                post=(d == 15 and wi == 1))
                    s = (2 * j + tb) % 2
                    t_act = self.op("act", lambda e, pp=pp, s=s: e.activation(
                        out=self.SG[s], in_=pp[:, 0, :], func=AF.Silu), waits=[t_pe, self.sg_free[s]])
                    t_dve = self.op("dve", lambda e, pp=pp, s=s, j=j, sl=sl: e.tensor_tensor(
                        out=AT[:, j, sl], in0=self.SG[s], in1=pp[:, 1, :], op=ALU.mult), waits=[t_act])
                    self.sg_free[s] = t_dve
                    self.pair_free[k] = t_dve
                    at_done = t_dve
                if fc % 2 == 1:
                    self.wgu_free[slot] = t_pe
                    if next_f < DFF:
                        pend_wgu.append(self.load_wgu(wg, wu, next_f))
                        next_f += 256
            for q4 in range(4):
                wslot, wdtok = pend_wd.pop(0)
                for dq in range(4):
                    dc = q4 * 4 + dq
                    k, pp, pfree = self.get_pair()
                    t_pe = None
                    for tb in range(2):
                        sl = slice(tb * 512, (tb + 1) * 512)
                        for j in range(n):
                            t_pe = self.op("pe", lambda e, pp=pp, tb=tb, wslot=wslot, j=j, dq=dq, sl=sl, n=n: e.matmul(
                                pp[:, tb, :], self.WD[wslot][:, j, dq * 128:(dq + 1) * 128], AT[:, j, sl],
                                start=(j == 0), stop=(j == n - 1)),
                                waits=[wdtok, at_done, pfree] if (j == 0 and tb == 0) else (),
                                post=(j == n - 1 and tb == 1))
                    t_dve = self.op("dve", lambda e, pp=pp, dc=dc: e.scalar_tensor_tensor(
                        out=X[:, dc, :], in0=pp.rearrange("p a b -> p (a b)"), scalar=0.5, in1=X[:, dc, :],
                        op0=ALU.mult, op1=ALU.add), waits=[t_pe])
                    self.pair_free[k] = t_dve
                    last_x = t_dve
                    if fuse_stats and gidx == len(FGROUPS) - 1:
                        if dc == 0:
                            self.stats_begin()
                        self.stats_chunk(dc, t_dve)
                self.wd_free[wslot] = t_pe
                if q4 + 2 < 4:
                    pend_wd.append(self.load_wd(wd, c0, n, (q4 + 2) * 512))
                elif gidx + 1 < len(FGROUPS):
                    nc0, nn = FGROUPS[gidx + 1]
                    pend_wd.append(self.load_wd(wd, nc0, nn, (q4 - 2) * 512))
        assert not pend_wgu and not pend_wd
        self.fused_stats = self.stats_finish() if fuse_stats else None
        return last_x

    def load_pair(self, apA, apB):
        s = self.wgu_i % 2
        self.wgu_i += 1
        self.dma("pool", self.WG[s], apA.rearrange("(c p) f -> p c f", p=128), self.s_wgu[s],
                 waits=[self.wgu_free[s]])
        t = self.dma("pool", self.WU[s], apB.rearrange("(c p) f -> p c f", p=128), self.s_wgu[s])
        return s, t

    def setup_stg(self):
        self.STG = [self.alloc(1024) for _ in range(2)]
        self.s_stg = [self.sem("stg%d" % i) for i in range(2)]
        self.stg_free = [None, None]
        self.stg_i = getattr(self, "stg_i", 0)
        self.vstg_free = [None, None]
        self.vstg_i = getattr(self, "vstg_i", 0)

    def proj(self, R, w, ngroups, r_ready, evac):
        pend = [self.load_pair(w[:, 0:256], w[:, 256:512])]
        t_pe = None
        for gi in range(ngroups):
            if gi + 1 < ngroups:
                c0 = (gi + 1) * 512
                pend.append(self.load_pair(w[:, c0:c0 + 256], w[:, c0 + 256:c0 + 512]))
            slot, wtok = pend.pop(0)
            for q in range(4):
                W = self.WG if q < 2 else self.WU
                k, pp, pfree = self.get_pair()
                for tb in range(2):
                    sl = slice(tb * 512, (tb + 1) * 512)
                    for d in range(16):
                        t_pe = self.op("pe", lambda e, W=W, slot=slot, d=d, q=q, sl=sl, tb=tb, pp=pp: e.matmul(
                            pp[:, tb, :], W[slot][:, d, (q % 2) * 128:(q % 2) * 128 + 128], R[:, d, sl],
                            start=(d == 0), stop=(d == 15)),
                            waits=[wtok, r_ready, pfree] if (d == 0 and tb == 0) else (),
                            post=(d == 15 and tb == 1))
                self.pair_free[k] = evac(gi * 4 + q, pp, t_pe)
            self.wgu_free[slot] = t_pe
        return t_pe

    def evac_to_dram(self, pp, t_pe, dst, bf16):
        s = self.stg_i % 2
        self.stg_i += 1
        eng = "act" if s == 0 else "dve"
        stg = self.STG[s]
        if bf16:
            stg = stg.bitcast(BF16)[:, 0:1024]
        src = pp.rearrange("p a b -> p (a b)")
        if eng == "act":
            t = self.op("act", lambda e, stg=stg, src=src: e.activation(out=stg, in_=src, func=AF.Copy),
                        waits=[t_pe, self.stg_free[s]])
        else:
            t = self.op("dve", lambda e, stg=stg, src=src: e.tensor_copy(out=stg, in_=src),
                        waits=[t_pe, self.stg_free[s]])
        self.stg_free[s] = self.dma("sp", dst, stg, self.s_stg[s], waits=[t])
        return t

    def final_norm_tile(self, gain, x_ready, dst, tok0, pre_stats=None):
        X, ACC, RSTD = self.X, self.ACC, self.RSTD
        t_r = pre_stats if pre_stats is not None else self.norm_stats(x_ready)
        last = None
        for c in range(16):
            s = self.stg_i % 2
            self.stg_i += 1
            stg = self.STG[s]
            t = self.op("dve", lambda e, c=c, stg=stg: e.scalar_tensor_tensor(
                out=stg, in0=X[:, c, :], scalar=gain[:, c:c + 1],
                in1=RSTD.rearrange("p a b -> p (a b)"), op0=ALU.mult, op1=ALU.mult),
                waits=[t_r, self.stg_free[s]])
            self.stg_free[s] = self.dma("sp", dst[c * 128:(c + 1) * 128, tok0:tok0 + TT], stg,
                                        self.s_stg[s], waits=[t])
            last = t
        return last

    def stats_begin(self):
        self._st_last = [None, None]

    def stats_chunk(self, c, x_tok):
        X, ACC = self.X, self.ACC
        for tb in range(2):
            sl = slice(tb * 512, (tb + 1) * 512)
            if c == 0:
                self._st_last[tb] = self.op("act", lambda e, c=c, sl=sl, tb=tb: e.activation(
                    out=ACC[:, tb, :], in_=X[:, c, sl], func=AF.Square),
                    waits=[x_tok, self.acc_free])
            else:
                s = self._st_i % 2
                self._st_i += 1
                t_sq = self.op("act", lambda e, c=c, sl=sl, s=s: e.activation(
                    out=self.SQ[s], in_=X[:, c, sl], func=AF.Square),
                    waits=[x_tok, self.sq_free[s]])
                last = self.op("dve", lambda e, tb=tb, s=s: e.tensor_tensor(
                    out=ACC[:, tb, :], in0=self.SQ[s], in1=ACC[:, tb, :], op=ALU.add),
                    waits=[t_sq, self._st_last[tb]])
                self.sq_free[s] = last
                self._st_last[tb] = last

    def stats_finish(self):
        ACC, RSTD = self.ACC, self.RSTD
        k, pp, pfree = self.get_pair()
        t_pe = None
        for tb in range(2):
            t_pe = self.op("pe", lambda e, tb=tb, pp=pp: e.matmul(
                pp[:, tb, :], self.ones, ACC[:, tb, :], start=True, stop=True),
                waits=[self._st_last[0], self._st_last[1], pfree, self.tok_ones])
        self.acc_free = t_pe
        t_act = self.op("act", lambda e, pp=pp: e.activation(
            out=RSTD, in_=pp, func=AF.Sqrt, bias=self.eps_ap, scale=1.0 / D),
            waits=[t_pe, self.rstd_free])
        self.pair_free[k] = t_act
        t_r = self.op("dve", lambda e: e.reciprocal(out=RSTD, in_=RSTD), waits=[t_act])
        return t_r

    def norm_stats(self, x_ready):
        self.stats_begin()
        for c in range(16):
            self.stats_chunk(c, x_ready)
        return self.stats_finish()

    def norm_tile(self, gain, x_ready, pre_stats=None):
        X, H, RSTD = self.X, self.H, self.RSTD
        t_r = pre_stats if pre_stats is not None else self.norm_stats(x_ready)
        t_h = None
        for c in range(16):
            t_h = self.op("dve", lambda e, c=c: e.scalar_tensor_tensor(
                out=H[:, c, :], in0=X[:, c, :], scalar=gain[:, c:c + 1],
                in1=RSTD.rearrange("p a b -> p (a b)"), op0=ALU.mult, op1=ALU.mult),
                waits=[t_r, x_ready, self.h_free])
        self.rstd_free = t_h
        return t_h

    def inproj_tile(self, w_in, h_ready, qT, kT, V, zr, tok0, need_q=True, need_zr=True, mid_cb=None):
        H = self.H

        def evac(chunk, pp, t_pe, base):
            zc = base + chunk
            if zc < 8:
                dst, bf = qT[zc * 128:(zc + 1) * 128, tok0:tok0 + TT], True
            elif zc < 16:
                dst, bf = kT[(zc - 8) * 128:(zc - 7) * 128, tok0:tok0 + TT], True
            else:
                dst, bf = zr[(zc - 24) * 128:(zc - 23) * 128, tok0:tok0 + TT], False
            return self.evac_to_dram(pp, t_pe, dst, bf)

        if need_q:
            t1 = self.proj(H, w_in[:, 0:2048], 4, h_ready, lambda c, pp, t: evac(c, pp, t, 0))
        else:
            t1 = self.proj(H, w_in[:, 1024:2048], 2, h_ready, lambda c, pp, t: evac(c, pp, t, 8))
        t_pe = None
        for vi in range(2):
            c0 = 2048 + vi * 512
            slot, wtok = self.load_pair(w_in[:, c0:c0 + 256], w_in[:, c0 + 256:c0 + 512])
            for tkb in range(TT // 128):
                k, pp, pfree = self.get_pair()
                for half, W in enumerate((self.WG, self.WU)):
                    for d in range(16):
                        t_pe = self.op("pe", lambda e, W=W, slot=slot, d=d, tkb=tkb, half=half, pp=pp: e.matmul(
                            pp[:, half, 0:256], H[:, d, tkb * 128:(tkb + 1) * 128], W[slot][:, d, :],
                            start=(d == 0), stop=(d == 15)),
                            waits=[wtok, h_ready, pfree] if (d == 0 and half == 0) else (),
                            post=(d == 15 and half == 1))
                s = self.vstg_i % 2
                self.vstg_i += 1
                stg = self.VSTG[s]
                t = None
                for half in range(2):
                    o_ = stg[:, 4 * half:4 * half + 4, 0:64]
                    i_ = pp[:, half, 0:256].rearrange("p (h c) -> p h c", c=64)
                    if half == 0:
                        t = self.op("act", lambda e, o_=o_, i_=i_: e.activation(out=o_, in_=i_, func=AF.Copy),
                                    waits=[t_pe, self.vstg_free[s], self.tok_vstg])
                    else:
                        t = self.op("dve", lambda e, o_=o_, i_=i_: e.tensor_copy(out=o_, in_=i_),
                                    waits=[t_pe, self.vstg_free[s], self.tok_vstg, t])
                dst = V[tok0 + tkb * 128:tok0 + (tkb + 1) * 128, vi * 520:(vi + 1) * 520]
                self.vstg_free[s] = self.dma("sp", dst, stg.rearrange("p h c -> p (h c)"), self.sem("vstg%d" % s),
                                             waits=[t])
                self.pair_free[k] = t
            self.wgu_free[slot] = t_pe
        if mid_cb is not None:
            mid_cb()
        if not need_zr:
            return t_pe
        t3 = self.proj(H, w_in[:, 3072:5120], 4, h_ready, lambda c, pp, t: evac(c, pp, t, 24))
        return t3

    def convpool_stage(self, zr, zrh, CW, PSC, PW, IC16, yT):
        E = 16
        prev = [None]

        def dv(fn, waits=()):
            t = self.op("dve", fn, waits=list(waits) + [prev[0]])
            prev[0] = t
            return t
        GB = self.alloc_f32(1, NT)[:, 0, :]
        GC = self.alloc(NT + E)
        CI = self.alloc(NT + E)
        T1 = self.alloc(NT)
        YC = [self.alloc(NT // 2).bitcast(BF16) for _ in range(2)]
        SB = self.alloc(NT + E)
        s_ld = self.sem("cp_ld")
        s_st = [self.sem("cp_st0"), self.sem("cp_st1")]
        yc_free = [None, None]
        yi = 0
        free = None
        for cc in range(4):
            self.dma("sp", GB, zr[cc * 128:(cc + 1) * 128, :], s_ld, waits=[free])
            self.dma("sp", GC[:, E:], zr[(4 + cc) * 128:(5 + cc) * 128, :], s_ld)
            self.dma("sp", GC[:, 0:E], zrh[cc * 128:(cc + 1) * 128, :], s_ld)
            self.dma("sp", CI[:, E:], zr[(8 + cc) * 128:(9 + cc) * 128, :], s_ld)
            ld = self.dma("sp", CI[:, 0:E], zrh[(4 + cc) * 128:(5 + cc) * 128, :], s_ld)
            dv(lambda e: e.tensor_tensor(out=GC, in0=GC, in1=CI, op=ALU.mult), waits=[ld])
            yield
            dv(lambda e, cc=cc: e.tensor_scalar(
                out=T1, in0=GC[:, E - 2:E - 2 + NT], scalar1=CW[:, cc, 0:1], scalar2=None, op0=ALU.mult))
            yield
            dv(lambda e, cc=cc: e.scalar_tensor_tensor(
                out=T1, in0=GC[:, E - 1:E - 1 + NT], scalar=CW[:, cc, 1:2], in1=T1, op0=ALU.mult, op1=ALU.add))
            yield
            dv(lambda e, cc=cc: e.scalar_tensor_tensor(
                out=T1, in0=GC[:, E:E + NT], scalar=CW[:, cc, 2:3], in1=T1, op0=ALU.mult, op1=ALU.add))
            yield
            s = yi % 2
            yi += 1
            t = dv(lambda e, s=s: e.tensor_tensor(out=YC[s], in0=T1, in1=GB, op=ALU.mult),
                   waits=[yc_free[s]])
            free = t
            yc_free[s] = self.dma("sp", yT[(8 + cc) * 128:(9 + cc) * 128, :], YC[s], s_st[s], waits=[t])
            yield
        PIN = GC
        SA = CI
        PB = T1.bitcast(BF16)[:, 0:NT]
        for g in range(4):
            w = 2 << g
            self.dma("sp", PIN[:, E:], zr[(12 + g) * 128:(13 + g) * 128, :], s_ld, waits=[free])
            ld = self.dma("sp", PIN[:, 0:E], zrh[(8 + g) * 128:(9 + g) * 128, :], s_ld)
            cur, bufs, look = PIN, [SA, SB], 0
            first = True
            for k in range(g + 1):
                step = 1 << k
                nl = look + step
                nxt = bufs[k % 2]
                dv(lambda e, cur=cur, nxt=nxt, nl=nl, look=look, step=step: e.tensor_tensor(
                    out=nxt[:, nl:], in0=cur[:, nl:], in1=cur[:, look:NT + E - step], op=ALU.add),
                    waits=[ld] if first else ())
                yield
                first = False
                cur, look = nxt, nl
            tmp = bufs[(g + 1) % 2]
            dv(lambda e, cur=cur, tmp=tmp, w=w: e.scalar_tensor_tensor(
                out=tmp[:, E:], in0=cur[:, E:], scalar=1.0 / w, in1=PIN[:, E:], op0=ALU.mult, op1=ALU.subtract))
            yield
            dv(lambda e, cur=cur, g=g: e.tensor_tensor(
                out=cur[:, E:2 * E], in0=cur[:, E:2 * E], in1=IC16[:, g, :], op=ALU.mult))
            dv(lambda e, cur=cur, tmp=tmp: e.tensor_tensor(
                out=tmp[:, E:2 * E], in0=cur[:, E:2 * E], in1=PIN[:, E:2 * E], op=ALU.subtract))
            t_pb = dv(lambda e, tmp=tmp: e.tensor_copy(out=PB, in_=tmp[:, E:]))
            free = t_pb
            yield
            s = yi % 2
            yi += 1
            t = None
            for qtr in range(4):
                k, pb, pfree = self.get_bank()
                c0_ = qtr * 512
                t_pe = self.op("pe", lambda e, g=g, pb=pb, c0_=c0_: e.matmul(
                    pb, PW[:, g, :], PB[:, c0_:c0_ + 512], start=True, stop=True),
                    waits=[t_pb, pfree, self.tok_pw])
                t = dv(lambda e, g=g, pb=pb, s=s, c0_=c0_: e.tensor_scalar(
                    out=YC[s][:, c0_:c0_ + 512], in0=pb,
                    scalar1=PSC[:, g:g + 1], scalar2=None, op0=ALU.mult), waits=[t_pe, yc_free[s]])
                self.bank_free[k] = t
                free = t_pe
                yield
            yc_free[s] = self.dma("sp", yT[(12 + g) * 128:(13 + g) * 128, :], YC[s], s_st[s], waits=[t])

    def get_bank(self):
        b_ = self.nbank % 8
        self.nbank += 1
        return b_, self.ps[:, b_, :], self.bank_free[b_]

    def attention_stage(self, qT, kT, kTh, V, Vh, MASK, yT, bg=None):
        QT = [self.alloc_bf16(1, NT)[:, 0, :] for _ in range(2)]
        KT = [self.alloc_bf16(1, 2 * NT)[:, 0, :] for _ in range(2)]
        V1 = [self.alloc_bf16(32, 130) for _ in range(2)]
        V4 = [self.alloc_bf16(32, 130) for _ in range(2)]
        V16 = [self.alloc_bf16(32, 130) for _ in range(2)]
        ACC = [self.alloc(NT) for _ in range(2)]
        SEL = self.alloc(64)
        self.op("dve", lambda e: e.memset(SEL[0:64, :], 0.0))
        t_sel = self.op("dve", lambda e: e.memset(SEL[64:65, :], 1.0))
        YH = [self.alloc_bf16(1, NT)[:, 0, :] for _ in range(2)]
        ET = [self.alloc_bf16(1, 256)[:, 0, :] for _ in range(8)]
        s_ld = [self.sem("at_ld0"), self.sem("at_ld1")]
        s_yh = [self.sem("at_yh0"), self.sem("at_yh1")]
        att_free = [None, None]
        acc_free = [None, None]
        yh_free = [None, None]
        et_free = [None] * 8
        bank_free = self.bank_free
        get_bank = self.get_bank
        pending = []
        fin_pe = [None]
        LT = [self.alloc(16) for _ in range(2)]
        s_lt = [self.sem("at_lt0"), self.sem("at_lt1")]
        nunit = [0]

        def bg_step():
            if bg is not None:
                next(bg, None)

        def issue_loads(hc):
            s = hc % 2
            rs_ = slice(hc * 128, (hc + 1) * 128)
            cs = slice(hc * 130, (hc + 1) * 130)
            w = [att_free[s]]
            self.dma("sp", QT[s], qT[rs_, :], s_ld[s], waits=w)
            self.dma("sp", KT[s][:, 0:NT], kTh[rs_, :], s_ld[s])
            self.dma("sp", KT[s][:, NT:], kT[rs_, :], s_ld[s])
            for kb, src in enumerate((Vh, V)):
                self.dma("sp", V1[s][:, 16 * kb:16 * kb + 16, :],
                         src[:, cs].rearrange("(b p) c -> p b c", p=128), s_ld[s])
                self.dma("sp", V16[s][:, 16 * kb:16 * kb + 16, :],
                         src[:, cs].rearrange("(i r) c -> i r c", r=16), s_ld[s])
                v4 = src[:, cs].rearrange("(kb i r) c -> r i kb c", i=128, r=4)
                for r in range(4):
                    t = self.dma("sp", V4[s][:, r * 8 + 4 * kb:r * 8 + 4 * kb + 4, :], v4[r], s_ld[s])
            return t

        ld_tok = {0: issue_loads(0)}
        hidx = 0
        ei = 0
        for hc in range(8):
            s = hc % 2
            if hc + 1 < 8:
                ld_tok[hc + 1] = issue_loads(hc + 1)
            ld = ld_tok[hc]
            last_pe = None
            for hd in range(2):
                a = hidx % 2
                hidx += 1
                ps_ = slice(hd * 64, (hd + 1) * 64)
                units = []
                for (st, Vb) in ((1, V1[s]), (4, V4[s]), (16, V16[s])):
                    nblk = 16 // st
                    for r in range(st):
                        for b in range(nblk):
                            Bq = nblk + b
                            qs = slice(r + st * 128 * b, r + st * 128 * b + st * 127 + 1, st)
                            kts, vts = [], []
                            for kb in (Bq - 1, Bq):
                                kts.append(slice(r + st * 128 * kb, r + st * 128 * kb + st * 127 + 1, st))
                                if st == 1:
                                    vts.append(kb)
                                elif st == 4:
                                    vts.append(r * 8 + kb)
                                else:
                                    vts.append(kb * 16 + r)
                            units.append((st, qs, kts, vts, Vb, b == 0))
                state = {}

                def emit_S(n):
                    nonlocal ei
                    st, qs, kts, vts, Vb, first = units[n]
                    k, pp, pfree = get_bank()
                    t_s = None
                    m = 1 if first else 0
                    for kbi in range(2):
                        t_s = self.op("pe", lambda e, pp=pp, kbi=kbi, kts=kts, qs=qs, s=s, ps_=ps_: e.matmul(
                            pp[:, kbi * 128:(kbi + 1) * 128], KT[s][ps_, kts[kbi]], QT[s][ps_, qs],
                            start=True, stop=True), waits=[ld, pfree] if kbi == 0 else (), post=(kbi == 1))
                    ex = ei % 8
                    ei += 1
                    t_e = self.op("act", lambda e, pp=pp, ex=ex: e.activation(
                        out=ET[ex], in_=pp[:, 0:256], func=AF.Exp, scale=0.125),
                        waits=[t_s, et_free[ex]])
                    meng = "dve" if n % 3 == 2 else "pool"
                    t_m = self.op(meng, lambda e, ex=ex, m=m: e.tensor_tensor(
                        out=ET[ex], in0=ET[ex], in1=MASK[:, m, :], op=ALU.mult), waits=[t_e, self.tok_mask])
                    state[n] = (k, pp, ex, t_m)

                def emit_NL(n):
                    st, qs, kts, vts, Vb, first = units[n]
                    k, pp, ex, t_m = state.pop(n)
                    t_nl = None
                    for kbi in range(2):
                        t_nl = self.op("pe", lambda e, pp=pp, kbi=kbi, vts=vts, Vb=Vb, ex=ex, hd=hd: e.matmul(
                            pp[0:65, 256:384], Vb[:, vts[kbi], hd * 65:hd * 65 + 65],
                            ET[ex][:, kbi * 128:(kbi + 1) * 128],
                            start=(kbi == 0), stop=(kbi == 1)), waits=[t_m] if kbi == 0 else (), post=(kbi == 1))
                    et_free[ex] = t_nl
                    src = pp[0:65, 256:384]
                    if st == 1:
                        t_acc = self.op("dve", lambda e, src=src, qs=qs, a=a: e.tensor_copy(
                            out=ACC[a][0:65, qs], in_=src), waits=[t_nl, acc_free[a]])
                    else:
                        t_acc = self.op("dve", lambda e, src=src, qs=qs, a=a: e.tensor_tensor(
                            out=ACC[a][0:65, qs], in0=src, in1=ACC[a][0:65, qs], op=ALU.add),
                            waits=[t_nl])
                    bank_free[k] = t_acc
                    return t_nl, t_acc

                LA = 5
                t_acc = None
                for n in range(len(units) + LA):
                    if n < len(units):
                        emit_S(n)
                    if n >= LA:
                        last_pe, t_acc = emit_NL(n - LA)
                    nunit[0] += 1
                    if nunit[0] % 8 == 0:
                        bg_step()
                    if nunit[0] % 8 == 4:
                        for g_ in pending:
                            next(g_, None)
                def finalize(a=a, hc=hc, hd=hd, t_acc=t_acc):
                    d1 = self.dma("sp", self.lscr[a:a + 1, :], ACC[a][64:65, :], s_lt[a], waits=[t_acc])
                    d2 = self.dma("sp", LT[a], self.lscr[a].rearrange("(p j) -> p j", j=16), s_lt[a], waits=[d1])
                    yield
                    t_rc = self.op("dve", lambda e: e.reciprocal(out=LT[a], in_=LT[a]), waits=[d2])
                    d3 = self.dma("sp", self.lscr2[a].rearrange("(p j) -> p j", j=16), LT[a], s_lt[a], waits=[t_rc])
                    d4 = self.dma("sp", ACC[a][64:65, :], self.lscr2[a:a + 1, :], s_lt[a], waits=[d3])
                    yield
                    t_y = None
                    for qtr in range(4):
                        k, pp, pfree = get_bank()
                        c0_ = qtr * 512
                        t_b = self.op("pe", lambda e, pp=pp, c0_=c0_: e.matmul(
                            pp[0:64, :], SEL[0:65, :], ACC[a][0:65, c0_:c0_ + 512], start=True, stop=True),
                            waits=[d4, pfree, t_sel])
                        t_y = self.op("dve", lambda e, pp=pp, c0_=c0_: e.tensor_tensor(
                            out=YH[a][0:64, c0_:c0_ + 512], in0=ACC[a][0:64, c0_:c0_ + 512],
                            in1=pp[0:64, :], op=ALU.mult),
                            waits=[t_b, yh_free[a]])
                        bank_free[k] = t_y
                        yield
                    acc_free[a] = t_y
                    yh_free[a] = self.dma("sp", yT[hc * 128 + hd * 64:hc * 128 + hd * 64 + 64, :], YH[a][0:64, :],
                                          s_yh[a], waits=[t_y])
                    fin_pe[0] = t_b
                for g_ in pending:
                    for _ in g_:
                        pass
                pending.clear()
                pending.append(finalize())
            att_free[s] = last_pe
        for g_ in pending:
            for _ in g_:
                pass
        if bg is not None:
            for _ in bg:
                pass

    def load_y_tile(self, yT, tok0):
        t = None
        for i in range(4):
            t = self.dma("sp", self.H[:, 4 * i:4 * i + 4, :],
                         yT[i * 512:(i + 1) * 512, tok0:tok0 + TT].rearrange("(c p) t -> p c t", p=128),
                         self.sem("yld"), waits=[self.h_free] if i == 0 else ())
        return t

    def outproj_tile(self, w_out, x_ready, y_ready):
        X = self.X
        last = [None]

        def evac(chunk, pp, t_pe):
            t = self.op("dve", lambda e, pp=pp, chunk=chunk: e.tensor_tensor(
                out=X[:, chunk, :], in0=pp.rearrange("p a b -> p (a b)"), in1=X[:, chunk, :], op=ALU.add),
                waits=[t_pe, x_ready])
            last[0] = t
            if chunk == 0:
                self.stats_begin()
            self.stats_chunk(chunk, t)
            return t
        t_pe = self.proj(self.H, w_out, 4, y_ready, evac)
        self.h_free = t_pe
        self.fused_stats = self.stats_finish()
        return last[0]


def _stage_A(B, regions, xsrc, x1, qT, kT, V, zr, G1, GM, wg, wu, wd, w_in, need_q):
    tiles = [(r, t) for r in regions for t in range(NT // TT)]
    xr = B.load_x(xsrc[tiles[0][0]], tiles[0][1] * TT)
    pre = None
    for i, (r, t) in enumerate(tiles):
        done = B.ffn_tile(G1, wg, wu, wd, xr, pre_stats=pre, fuse_stats=True)
        st = B.store_x(x1[r], t * TT, [done])
        h_ready = B.norm_tile(GM, done, pre_stats=B.fused_stats)
        B.x_free = [st, h_ready]
        nxt = {}
        if i + 1 < len(tiles):
            nxt["xr"] = B.load_x(xsrc[tiles[i + 1][0]], tiles[i + 1][1] * TT)

            def mid(nxt=nxt):
                nxt["pre"] = B.norm_stats(nxt["xr"])
        else:
            mid = None
        B.h_free = B.inproj_tile(w_in, h_ready, qT[r], kT[r], V[r], zr[r], t * TT, need_q=need_q[r],
                                 need_zr=(need_q[r] or t == NT // TT - 1), mid_cb=mid)
        xr, pre = nxt.get("xr"), nxt.get("pre")
        B.x_free = []


def _stage_B(B, base, r, x1, qT, kT, V, zr, yT, x3, outT, CW, PSC, PW, IC16, MASK, G2, GF,
             w_out, wg, wu, wd, final):
    B.full_barrier()
    B.aoff = base
    zrh = zr[r + 1][512:2048, NT - 16:NT]
    B.bank_free = [None] * 8
    B.nbank = 0
    bg = B.convpool_stage(zr[r], zrh, CW, PSC, PW, IC16, yT)
    B.attention_stage(qT[r], kT[r], kT[r + 1], V[r], V[r + 1], MASK, yT, bg=bg)
    B.full_barrier()
    B.aoff = base
    B.setup_ffn_bufs()
    B.setup_stg()
    for t in range(NT // TT):
        xr = B.load_x(x1[r], t * TT)
        yr = B.load_y_tile(yT, t * TT)
        x2 = B.outproj_tile(w_out, xr, yr)
        done = B.ffn_tile(G2, wg, wu, wd, x2, pre_stats=B.fused_stats, fuse_stats=final)
        B.h_free = done
        if final:
            fin = B.final_norm_tile(GF, done, outT, t * TT, pre_stats=B.fused_stats)
            B.x_free = [fin]
        else:
            st = B.store_x(x3[r], t * TT, [done])
            B.x_free = [st]


NCONST = 64 + 24 + 8 + 128 + 16


def build_fused():
    nc = bass.Bass("TRN2", target_bir_lowering=False)
    xin = nc.dram_tensor("xin", [3, D, NT], F32, kind="ExternalInput").ap()
    consts = nc.dram_tensor("consts", [128, NCONST + 32], F32, kind="ExternalInput").ap()
    masks = nc.dram_tensor("masks", [128, 4, 256], BF16, kind="ExternalInput").ap()
    W = []
    for l in range(2):
        d = {}
        for nm, shp in (("wg1", [D, DFF]), ("wu1", [D, DFF]), ("wd1", [DFF, D]), ("w_in", [D, DIN]),
                        ("w_out", [D, D]), ("wg2", [D, DFF]), ("wu2", [D, DFF]), ("wd2", [DFF, D]),
                        ("pool_w", [4, 128, 128])):
            d[nm] = nc.dram_tensor("%s_%d" % (nm, l), shp, F32, kind="ExternalInput").ap()
        W.append(d)
    outT = nc.dram_tensor("outT", [D, NT], F32, kind="ExternalOutput").ap()
    x1 = [nc.dram_tensor("x1_%d" % r, [D, NT], F32).ap() for r in range(3)]
    x3 = [nc.dram_tensor("x3_%d" % r, [D, NT], F32).ap() for r in range(2)]
    qT = [nc.dram_tensor("q_%d" % r, [1024, NT], BF16).ap() for r in range(3)]
    kT = [nc.dram_tensor("k_%d" % r, [1024, NT], BF16).ap() for r in range(3)]
    V = [nc.dram_tensor("v_%d" % r, [NT, 1040], BF16).ap() for r in range(3)]
    zr = [nc.dram_tensor("zr_%d" % r, [2048, NT], F32).ap() for r in range(3)]
    yT = nc.dram_tensor("yT", [D, NT], BF16).ap()
    lscr = nc.dram_tensor("lscr", [2, NT], F32).ap()
    lscr2 = nc.dram_tensor("lscr2", [2, NT], F32).ap()
    xsrc = [xin[r] for r in range(3)]
    with ExitStack() as stack:
        B = Builder(nc, stack)
        B.lscr, B.lscr2 = lscr, lscr2
        B.x_free = []
        B.acc_free = None
        B.rstd_free = None
        B.h_free = None
        B.eps_ap = B.alloc(1)
        t_eps = B.op("dve", lambda e: e.memset(B.eps_ap, EPS))
        B.wait_only("act", [t_eps])
        C = B.alloc(NCONST + 32)
        MASK = B.alloc_bf16(4, 256)
        PW = [B.alloc_bf16(4, 128) for _ in range(2)]
        s_c = B.sem("consts")
        B.dma("sp", C, consts[:, :], s_c)
        B.tok_mask = B.dma("sp", MASK, masks[:, :, :], s_c)
        B.wait_only("dve", [B.tok_mask])
        s_pw = B.sem("pw")
        for l in range(2):
            B.tok_pw = B.dma("pool", PW[l], W[l]["pool_w"].rearrange("g c d -> c g d"), s_pw)
        G = lambda i: C[:, 16 * i:16 * i + 16]
        CWs = [C[:, 112 + 12 * l:112 + 12 * l + 12].rearrange("p (a b) -> p a b", a=4) for l in range(2)]
        PSCs = [C[:, 136 + 4 * l:136 + 4 * l + 4] for l in range(2)]
        IC = [C[:, 144 + 64 * k:144 + 64 * k + 64].rearrange("p (a b) -> p a b", a=4) for k in range(2)]
        MK = [MASK[:, 2 * k:2 * k + 2, :] for k in range(2)]
        base = B.aoff
        B.setup_ffn_bufs()
        B.setup_stg()
        _stage_A(B, [2, 1, 0], xsrc, x1, qT, kT, V, zr, G(0), G(1), W[0]["wg1"], W[0]["wu1"], W[0]["wd1"],
                 W[0]["w_in"], need_q={2: False, 1: True, 0: True})
        for r in (1, 0):
            _stage_B(B, base, r, x1, qT, kT, V, zr, yT, x3, None, CWs[0], PSCs[0], PW[0], IC[r], MK[r],
                     G(2), None, W[0]["w_out"], W[0]["wg2"], W[0]["wu2"], W[0]["wd2"], final=False)
        B.full_barrier()
        _stage_A(B, [1, 0], x3, x1, qT, kT, V, zr, G(3), G(4), W[1]["wg1"], W[1]["wu1"], W[1]["wd1"],
                 W[1]["w_in"], need_q={1: False, 0: True})
        _stage_B(B, base, 0, x1, qT, kT, V, zr, yT, x3, outT, CWs[1], PSCs[1], PW[1], IC[0], MK[0],
                 G(5), G(6), W[1]["w_out"], W[1]["wg2"], W[1]["wu2"], W[1]["wd2"], final=True)
        B.full_barrier()
        B.emit()
    return nc


_CACHE = {}


def _chunked(v):
    return np.ascontiguousarray(np.asarray(v, np.float32).reshape(-1, 128).T)


def kernel(x, ffn1_norm, ffn1_w_gate, ffn1_w_up, ffn1_w_down, mix_norm, w_in, conv_w,
           pool_w, pool_scale, w_out, ffn2_norm, ffn2_w_gate, ffn2_w_up, ffn2_w_down, final_norm):
    f = lambda a: np.ascontiguousarray(np.asarray(a, np.float32))
    x = f(x)
    cores = list(range(NCORES))
    if "nc" not in _CACHE:
        _CACHE["nc"] = build_fused()
    nc = _CACHE["nc"]
    jj = np.arange(128)[:, None]
    ii = np.arange(128)[None, :]
    m_prev = (jj >= ii).astype(np.float32)
    m_cur = (jj <= ii).astype(np.float32)
    wts = {}
    for l in range(2):
        wts.update({"wg1_%d" % l: f(ffn1_w_gate[l]), "wu1_%d" % l: f(ffn1_w_up[l]), "wd1_%d" % l: f(ffn1_w_down[l]),
                    "w_in_%d" % l: f(w_in[l]), "w_out_%d" % l: f(w_out[l]), "wg2_%d" % l: f(ffn2_w_gate[l]),
                    "wu2_%d" % l: f(ffn2_w_up[l]), "wd2_%d" % l: f(ffn2_w_down[l]), "pool_w_%d" % l: f(pool_w[l])})
    gains = [ffn1_norm[0], mix_norm[0], ffn2_norm[0], ffn1_norm[1], mix_norm[1], ffn2_norm[1], final_norm]
    cws = []
    for l in range(2):
        cw = np.asarray(conv_w[l], np.float32)
        cws.append(np.ascontiguousarray(cw.T.reshape(4, 128, 3).transpose(1, 0, 2)).reshape(128, 12))
    ins = []
    for c in cores:
        b, p = c // 4, c % 4
        xin = np.zeros((3, D, NT), np.float32)
        for r in range(3):
            if p - r >= 0:
                xin[r] = x[b, (p - r) * NT:(p - r + 1) * NT, :].T
        mk = np.zeros((128, 4, 256), np.float32)
        ics = []
        for r in range(2):
            halo_virtual = (p - r - 1) < 0
            mk[:, 2 * r, 0:128] = m_prev
            mk[:, 2 * r, 128:256] = m_cur
            mk[:, 2 * r + 1, 0:128] = 0.0 if halo_virtual else m_prev
            mk[:, 2 * r + 1, 128:256] = m_cur
            pos = max(p - r, 0) * NT + np.arange(16)
            ics.append(np.stack([1.0 / np.minimum(pos + 1, w) for w in (2, 4, 8, 16)]).astype(np.float32).reshape(1, 64))
        cst = np.concatenate(
            [_chunked(g) for g in gains] + cws + [_chunked(pool_scale[0]), _chunked(pool_scale[1])]
            + [np.broadcast_to(ic, (128, 64)) for ic in ics], axis=1)
        assert cst.shape == (128, NCONST + 32), cst.shape
        ins.append(dict(xin=xin, consts=np.ascontiguousarray(cst.astype(np.float32)),
                        masks=mk.astype(ml_dtypes.bfloat16), **wts))
    res = run_bass_kernel_spmd(nc, ins, core_ids=cores).results
    y = np.empty((2, 4 * NT, D), np.float32)
    for c in cores:
        y[c // 4, (c % 4) * NT:(c % 4 + 1) * NT, :] = res[c]["outT"].T
    return y
```
